# Optimizing a Trainium2 kernel written in Bass

```python
import jax
import jax.numpy as jnp
from jax import lax
import numpy as np

D_MODEL = 1024
BATCH = 8
SEQ = 4096
DEPTH = 4

GRID_W = 64
HEAD_DIM = 64
LRU_BLOCKS = 6
LRU_BLOCK_DIM = 64
LRU_WIDTH = LRU_BLOCKS * LRU_BLOCK_DIM
RET_HEADS = 6
RET_WIDTH = RET_HEADS * HEAD_DIM
NA_HEADS = 4
NA_WIDTH = NA_HEADS * HEAD_DIM
MIX_WIDTH = LRU_WIDTH + RET_WIDTH + NA_WIDTH
IN_WIDTH = 2 * LRU_WIDTH + 4 * RET_WIDTH + 3 * NA_WIDTH
CONV_WIDTH = 4
LRU_C = 8.0
LRU_A_MIN = 0.9
LRU_A_MAX = 0.999
RET_CHUNK = 128
ROPE_BASE = 10000.0
GN_EPS = 1e-6
NA_MAX_KH = 8
NA_KW = 16
NA_QB = 16
NA_KS = NA_QB + NA_KW
D_FF = -(-8 * D_MODEL // (3 * 256)) * 256
DEEPNORM_ALPHA = (2 * DEPTH) ** 0.25
DEEPNORM_BETA = (8 * DEPTH) ** -0.25
LN_EPS = 1e-5

kernel_name = "hybrid_lru_retention_natten_deepnorm_encoder"


def _layer_norm(x, g, b):
    xf = x.astype(jnp.float32)
    mu = jnp.mean(xf, axis=-1, keepdims=True)
    xc = xf - mu
    var = jnp.mean(xc * xc, axis=-1, keepdims=True)
    y = xc * lax.rsqrt(var + LN_EPS) * g.astype(jnp.float32) + b.astype(jnp.float32)
    return y.astype(x.dtype)


def _linear_combine(c1, c2):
    a1, b1 = c1
    a2, b2 = c2
    return a1 * a2, a2 * b1 + b2


def _rg_lru_group(xb, gate, conv_w, conv_b, w_a, b_a, w_x, b_x, lam):
    bsz, seq, _ = xb.shape
    left = CONV_WIDTH // 2
    xp = jnp.pad(xb.astype(jnp.float32), ((0, 0), (left, CONV_WIDTH - 1 - left), (0, 0)))
    cw = conv_w.astype(jnp.float32)
    xc = conv_b.astype(jnp.float32)
    for j in range(CONV_WIDTH):
        xc = xc + xp[:, j:j + seq] * cw[j]
    xg = xc.reshape(bsz, seq, LRU_BLOCKS, LRU_BLOCK_DIM)
    h_sum = jnp.zeros_like(xc)
    for d, rev in enumerate((False, True)):
        r = jax.nn.sigmoid(jnp.einsum('bsgi,gij->bsgj', xg, w_a[d].astype(jnp.float32)).reshape(bsz, seq, LRU_WIDTH) + b_a[d].astype(jnp.float32))
        i = jax.nn.sigmoid(jnp.einsum('bsgi,gij->bsgj', xg, w_x[d].astype(jnp.float32)).reshape(bsz, seq, LRU_WIDTH) + b_x[d].astype(jnp.float32))
        log_a = -LRU_C * r * jax.nn.softplus(-lam[d].astype(jnp.float32))
        a = jnp.exp(log_a)
        u = jnp.sqrt(-jnp.expm1(2.0 * log_a)) * (i * xc)
        _, h = lax.associative_scan(_linear_combine, (a, u), axis=1, reverse=rev)
        h_sum = h_sum + h
    return (h_sum * jax.nn.gelu(gate.astype(jnp.float32))).astype(xb.dtype)


def _rope(t, pos):
    half = HEAD_DIM // 2
    inv_freq = ROPE_BASE ** (-jnp.arange(half, dtype=jnp.float32) / half)
    ang = pos[:, None] * inv_freq[None, :]
    cos = jnp.cos(ang)[None, :, None, :]
    sin = jnp.sin(ang)[None, :, None, :]
    t1, t2 = t[..., :half], t[..., half:]
    return jnp.concatenate([t1 * cos - t2 * sin, t1 * sin + t2 * cos], axis=-1)


def _retention_group(q, k, v, g, gn_w):
    bsz, seq, _ = q.shape
    C = RET_CHUNK
    nc = seq // C
    pos = jnp.arange(seq, dtype=jnp.float32)

    def heads(t):
        return t.astype(jnp.float32).reshape(bsz, seq, RET_HEADS, HEAD_DIM)

    def chunks(t):
        return t.transpose(0, 2, 1, 3).reshape(bsz, RET_HEADS, nc, C, HEAD_DIM)

    qc = chunks(_rope(heads(q), pos))
    kc = chunks(_rope(heads(k), pos) * HEAD_DIM ** -0.5)
    vc = chunks(heads(v))
    log_g = jnp.log1p(-jnp.exp2(-5.0 - jnp.arange(RET_HEADS, dtype=jnp.float32)))
    lg = log_g[:, None]
    idx = jnp.arange(C, dtype=jnp.float32)

    def bc(w):
        return w[:, None, :, None]

    decay = jnp.exp(jnp.abs(idx[:, None] - idx[None, :]) * log_g[:, None, None])
    s = jnp.einsum('bhnid,bhnjd->bhnij', qc, kc) * decay[:, None]
    o = jnp.einsum('bhnij,bhnjd->bhnid', s, vc)
    kv_f = jnp.einsum('bhnjd,bhnje->nbhde', kc * bc(jnp.exp((C - 1 - idx) * lg)), vc)
    kv_b = jnp.einsum('bhnjd,bhnje->nbhde', kc * bc(jnp.exp(idx * lg)), vc)
    g_chunk = jnp.exp(C * log_g)[:, None, None]

    def step(state, kv):
        return g_chunk * state + kv, state

    init = jnp.zeros((bsz, RET_HEADS, HEAD_DIM, HEAD_DIM), jnp.float32)
    _, st_f = lax.scan(step, init, kv_f)
    _, st_b = lax.scan(step, init, kv_b, reverse=True)
    o = (o
         + jnp.einsum('bhnid,nbhde->bhnie', qc * bc(jnp.exp((idx + 1.0) * lg)), st_f)
         + jnp.einsum('bhnid,nbhde->bhnie', qc * bc(jnp.exp((C - idx) * lg)), st_b))
    mu = jnp.mean(o, axis=-1, keepdims=True)
    oc = o - mu
    var = jnp.mean(oc * oc, axis=-1, keepdims=True)
    o = oc * lax.rsqrt(var + GN_EPS)
    o = o.reshape(bsz, RET_HEADS, seq, HEAD_DIM).transpose(0, 2, 1, 3).reshape(bsz, seq, RET_WIDTH)
    o = o * gn_w.astype(jnp.float32)
    return (jax.nn.silu(g.astype(jnp.float32)) * o).astype(q.dtype)


def _neighbourhood_attention_group(q, k, v, rpb):
    bsz, seq, _ = q.shape
    rows_n = seq // GRID_W
    kh = min(NA_MAX_KH, rows_n)
    nb = GRID_W // NA_QB
    rows = np.arange(rows_n)
    rstart = np.clip(rows - kh // 2, 0, rows_n - kh)
    row_idx = rstart[:, None] + np.arange(kh)
    dr = row_idx - rows[:, None]
    c0 = np.arange(nb) * NA_QB
    sstart = np.clip(c0 - NA_KW // 2, 0, GRID_W - NA_KS)
    col_idx = sstart[:, None] + np.arange(NA_KS)
    qcol = c0[:, None] + np.arange(NA_QB)
    cstart = np.clip(qcol - NA_KW // 2, 0, GRID_W - NA_KW)
    dc = col_idx[:, None, :] - qcol[:, :, None]
    valid = (col_idx[:, None, :] >= cstart[:, :, None]) & (col_idx[:, None, :] < cstart[:, :, None] + NA_KW)
    ri = (dr + NA_MAX_KH - 1)[:, None, None, :, None]
    ci = (np.clip(dc, 1 - NA_KW, NA_KW - 1) + NA_KW - 1)[None, :, :, None, :]
    bias = rpb.astype(jnp.float32)[:, ri, ci]
    bias = jnp.where(valid[None, None, :, :, None, :], bias, -jnp.inf)
    bias = bias.reshape(NA_HEADS, rows_n, nb, NA_QB, kh * NA_KS)

    def heads(t):
        return t.reshape(bsz, seq, NA_HEADS, HEAD_DIM).transpose(0, 2, 1, 3)

    qg = heads(q).reshape(bsz, NA_HEADS, rows_n, nb, NA_QB, HEAD_DIM)
    kgrid = heads(k).reshape(bsz, NA_HEADS, rows_n, GRID_W, HEAD_DIM)
    vgrid = heads(v).reshape(bsz, NA_HEADS, rows_n, GRID_W, HEAD_DIM)
    gr = row_idx[:, None, :, None]
    gc = col_idx[None, :, None, :]
    kb = kgrid[:, :, gr, gc].reshape(bsz, NA_HEADS, rows_n, nb, kh * NA_KS, HEAD_DIM)
    vb = vgrid[:, :, gr, gc].reshape(bsz, NA_HEADS, rows_n, nb, kh * NA_KS, HEAD_DIM)
    s = jnp.einsum('bhrnqd,bhrnkd->bhrnqk', qg, kb).astype(jnp.float32) * HEAD_DIM ** -0.5 + bias[None]
    p = jax.nn.softmax(s, axis=-1).astype(v.dtype)
    o = jnp.einsum('bhrnqk,bhrnkd->bhrnqd', p, vb)
    return o.reshape(bsz, NA_HEADS, seq, HEAD_DIM).transpose(0, 2, 1, 3).reshape(bsz, seq, NA_WIDTH)


def setup_inputs(seed: int = 0) -> dict:
    key = jax.random.key(seed)
    ks = jax.random.split(key, 20)

    def nrm(k, shape, scale):
        return jax.random.normal(k, shape, jnp.float32) * scale

    x = nrm(ks[0], (BATCH, SEQ, D_MODEL), 1.0)
    w_in = nrm(ks[1], (DEPTH, D_MODEL, IN_WIDTH), D_MODEL ** -0.5)
    conv_w = nrm(ks[2], (DEPTH, CONV_WIDTH, LRU_WIDTH), CONV_WIDTH ** -0.5)
    conv_b = nrm(ks[3], (DEPTH, LRU_WIDTH), 0.01)
    lru_w_a = nrm(ks[4], (DEPTH, 2, LRU_BLOCKS, LRU_BLOCK_DIM, LRU_BLOCK_DIM), LRU_BLOCK_DIM ** -0.5)
    lru_b_a = nrm(ks[5], (DEPTH, 2, LRU_WIDTH), 0.01)
    lru_w_x = nrm(ks[6], (DEPTH, 2, LRU_BLOCKS, LRU_BLOCK_DIM, LRU_BLOCK_DIM), LRU_BLOCK_DIM ** -0.5)
    lru_b_x = nrm(ks[7], (DEPTH, 2, LRU_WIDTH), 0.01)
    a_pow = jax.random.uniform(ks[8], (DEPTH, 2, LRU_WIDTH), jnp.float32, minval=LRU_A_MIN, maxval=LRU_A_MAX)
    a0 = a_pow ** (1.0 / LRU_C)
    lru_lam = jnp.log(a0) - jnp.log1p(-a0)
    ret_gn_w = 1.0 + nrm(ks[9], (DEPTH, RET_WIDTH), 0.02)
    na_rpb = nrm(ks[10], (DEPTH, NA_HEADS, 2 * NA_MAX_KH - 1, 2 * NA_KW - 1), 0.02)
    w_out = nrm(ks[11], (DEPTH, MIX_WIDTH, D_MODEL), MIX_WIDTH ** -0.5 * DEEPNORM_BETA)
    ln1_g = 1.0 + nrm(ks[12], (DEPTH, D_MODEL), 0.02)
    ln1_b = nrm(ks[13], (DEPTH, D_MODEL), 0.01)
    w_gate = nrm(ks[14], (DEPTH, D_MODEL, D_FF), D_MODEL ** -0.5)
    w_up = nrm(ks[15], (DEPTH, D_MODEL, D_FF), D_MODEL ** -0.5)
    w_down = nrm(ks[16], (DEPTH, D_FF, D_MODEL), D_FF ** -0.5 * DEEPNORM_BETA)
    ln2_g = 1.0 + nrm(ks[17], (DEPTH, D_MODEL), 0.02)
    ln2_b = nrm(ks[18], (DEPTH, D_MODEL), 0.01)
    return {'x': x, 'w_in': w_in, 'conv_w': conv_w, 'conv_b': conv_b,
            'lru_w_a': lru_w_a, 'lru_b_a': lru_b_a, 'lru_w_x': lru_w_x, 'lru_b_x': lru_b_x,
            'lru_lam': lru_lam, 'ret_gn_w': ret_gn_w, 'na_rpb': na_rpb, 'w_out': w_out,
            'ln1_g': ln1_g, 'ln1_b': ln1_b, 'w_gate': w_gate, 'w_up': w_up, 'w_down': w_down,
            'ln2_g': ln2_g, 'ln2_b': ln2_b}


def reference(x, w_in, conv_w, conv_b, lru_w_a, lru_b_a, lru_w_x, lru_b_x, lru_lam,
              ret_gn_w, na_rpb, w_out, ln1_g, ln1_b, w_gate, w_up, w_down, ln2_g, ln2_b):
    sizes = [LRU_WIDTH, LRU_WIDTH, RET_WIDTH, RET_WIDTH, RET_WIDTH, RET_WIDTH, NA_WIDTH, NA_WIDTH, NA_WIDTH]
    offsets = [int(o) for o in np.cumsum(sizes)[:-1]]
    for l in range(DEPTH):
        proj = x @ w_in[l]
        lru_x, lru_g, rq, rk, rv, rg, nq, nk, nv = jnp.split(proj, offsets, axis=-1)
        y_lru = _rg_lru_group(lru_x, lru_g, conv_w[l], conv_b[l], lru_w_a[l], lru_b_a[l],
                              lru_w_x[l], lru_b_x[l], lru_lam[l]).astype(proj.dtype)
        y_ret = _retention_group(rq, rk, rv, rg, ret_gn_w[l]).astype(proj.dtype)
        y_na = _neighbourhood_attention_group(nq, nk, nv, na_rpb[l]).astype(proj.dtype)
        mix = jnp.concatenate([y_lru, y_ret, y_na], axis=-1) @ w_out[l]
        x = _layer_norm(DEEPNORM_ALPHA * x + mix, ln1_g[l], ln1_b[l])
        hid = jax.nn.silu(x @ w_gate[l]) * (x @ w_up[l])
        x = _layer_norm(DEEPNORM_ALPHA * x + hid @ w_down[l], ln2_g[l], ln2_b[l])
    return x
```

```python
import contextlib
import numpy as np
import concourse.bass as bass
import concourse.mybir as mybir
from concourse.bass_utils import run_bass_kernel_spmd

F32 = mybir.dt.float32
BF16 = mybir.dt.bfloat16
AF = mybir.ActivationFunctionType
ALU = mybir.AluOpType

S = 4096
D = 1024
NT = 32
NB = 8
INW = 3072
DFF = 2816
NF = 22
DEPTH = 4
ALPHA = float((2 * DEPTH) ** 0.25)
LN_EPS = 1e-5
GN_EPS = 1e-6
NEG = -30000.0
NDS = 48
NHW = 24


class Buf:
    __slots__ = ("w", "r", "excl")

    def __init__(self, excl=False):
        self.w = []
        self.r = {}
        self.excl = excl


def bufs(n):
    return [Buf() for _ in range(n)]


def pbufs(n):
    return [Buf(True) for _ in range(n)]


class KB:
    def __init__(self, nc, es):
        self.nc = nc
        self.es = es
        self.E = {"pe": nc.tensor, "act": nc.scalar, "dve": nc.vector, "pool": nc.gpsimd, "sp": nc.sync}
        self.epoch = 0
        self.nsem = 10
        self.semsets = [{n: es.enter_context(nc.semaphore(f"e{k}_{n}")) for n in self.E} for k in range(2)]
        self.sem = dict(self.semsets[0])
        self.cnt = {n: 0 for n in self.E}
        self.seen = {n: {} for n in self.E}
        self.dsem = [es.enter_context(nc.semaphore(f"dq{i}")) for i in range(NDS)]
        self.dcnt = [0] * NDS
        self.dnext = 0
        self.dnext_sw = 0
        self.bar = es.enter_context(nc.semaphore("bar"))
        self.barcnt = 0
        self.ninstr = 0

    def wait(self, eng, tok):
        if tok is None:
            return
        key, sem, val, ep = tok
        if ep < self.epoch:
            return
        if key == "pe" and eng == "pe":
            return
        if self.seen[eng].get(key, 0) >= val:
            return
        self.E[eng].wait_ge(sem, val)
        self.seen[eng][key] = val

    def _deps(self, eng, reads, writes, accum=False):
        for b in reads:
            for t in b.w:
                self.wait(eng, t)
            if b.excl:
                for k, t in b.r.items():
                    if k != eng:
                        self.wait(eng, t)
        for b in writes:
            if not accum:
                for t in b.w:
                    self.wait(eng, t)
            for t in b.r.values():
                self.wait(eng, t)

    def _mark(self, tok, reads, writes, accum=False):
        for b in reads:
            b.r[tok[0]] = tok
        for b in writes:
            if accum:
                b.w = [t for t in b.w if t[3] >= self.epoch] + [tok]
            else:
                b.w = [tok]
            b.r = {}

    def op(self, eng, fn, reads=(), writes=()):
        self._deps(eng, reads, writes)
        ins = fn(self.E[eng])
        self.cnt[eng] += 1
        ins.then_inc(self.sem[eng], 1)
        tok = (eng, self.sem[eng], self.cnt[eng], self.epoch)
        self._mark(tok, reads, writes)
        self.ninstr += 1
        return tok

    def dma(self, q, out, in_, reads=(), writes=(), accum=False):
        if q == "pool":
            i = NHW + self.dnext_sw
            self.dnext_sw = (self.dnext_sw + 1) % (NDS - NHW)
        else:
            i = self.dnext
            self.dnext = (self.dnext + 1) % NHW
        if self.dcnt[i] > 0:
            self.wait(q, (("d", i), self.dsem[i], self.dcnt[i], self.epoch))
        self._deps(q, reads, writes, accum)
        ins = self.E[q].dma_start(out=out, in_=in_)
        self.dcnt[i] += 16
        ins.then_inc(self.dsem[i], 16)
        tok = (("d", i), self.dsem[i], self.dcnt[i], self.epoch)
        self._mark(tok, reads, writes, accum)
        self.ninstr += 1
        return tok

    def barrier(self):
        for w in self.E:
            for e in self.E:
                if e != w and self.cnt[e] > 0:
                    self.wait(w, (e, self.sem[e], self.cnt[e], self.epoch))
        for i in range(NDS):
            if self.dcnt[i] > 0:
                self.wait("sp", (("d", i), self.dsem[i], self.dcnt[i], self.epoch))
        nxt = self.semsets[(self.epoch + 1) % 2]
        self.nc.sync.sem_inc(self.bar, 1)
        self.barcnt += 1
        for e in self.E:
            if e != "sp":
                self.E[e].wait_ge(self.bar, self.barcnt)
        if self.epoch >= 1:
            self.nc.all_engine_barrier()
            for e in self.E:
                self.nc.sync.sem_clear(nxt[e])
            self.nc.sync.sem_inc(self.bar, 1)
            self.barcnt += 1
            for e in self.E:
                if e != "sp":
                    self.E[e].wait_ge(self.bar, self.barcnt)
        self.epoch += 1
        self.sem = dict(nxt)
        for e in self.E:
            self.cnt[e] = 0
            self.seen[e] = {}


def _ret_consts():
    h = np.arange(6, dtype=np.float32)
    log_g = np.log1p(-np.exp2(-5.0 - h)).astype(np.float32)
    idx = np.arange(128, dtype=np.float32)
    decayT = np.exp(np.abs(idx[:, None] - idx[None, :])[:, None, :] * log_g[None, :, None]) * 0.125
    wf = np.exp((127.0 - idx)[:, None] * log_g[None, :]) * 0.125
    wb = np.exp(idx[:, None] * log_g[None, :]) * 0.125
    w1 = np.exp((idx + 1.0)[:, None] * log_g[None, :])
    w2 = np.exp((128.0 - idx)[:, None] * log_g[None, :])
    rep = lambda w: np.repeat(w[:, :, None], 64, axis=2)
    wtab = np.stack([rep(wf), rep(wb), rep(w1), rep(w2)], axis=1)
    gch = np.exp(128.0 * log_g)
    gcol = np.zeros((128, 3), np.float32)
    for hp in range(3):
        gcol[:64, hp] = gch[2 * hp]
        gcol[64:, hp] = gch[2 * hp + 1]
    half = 32
    inv_freq = (10000.0 ** (-np.arange(half, dtype=np.float32) / half)).astype(np.float32)
    pos = np.arange(S, dtype=np.float32)
    ang = (pos[:, None] * inv_freq[None, :]).astype(np.float32)
    cos = np.cos(ang).astype(np.float32).reshape(32, 128, 32).transpose(1, 0, 2)
    sin = np.sin(ang).astype(np.float32).reshape(32, 128, 32).transpose(1, 0, 2)
    cc = np.concatenate([cos, cos], axis=2)
    ss = np.concatenate([-sin, sin], axis=2)
    rope = np.stack([cc, ss], axis=1)
    return (np.ascontiguousarray(decayT, np.float32), np.ascontiguousarray(wtab, np.float32),
            gcol, np.ascontiguousarray(rope, np.float32))


def _na_plan():
    R = 64
    rows = np.arange(R)
    rstart = np.clip(rows - 4, 0, R - 8)
    qc = np.arange(64)
    cstart = np.clip(qc - 8, 0, 48)
    kc = np.arange(64)
    colvalid = (kc[:, None] >= cstart[None, :]) & (kc[:, None] < cstart[None, :] + 16)
    dcidx = np.clip(kc[:, None] - qc[None, :], -15, 15) + 15
    blocks = {}
    blk_list = []
    pair_blk = {}
    win = []
    for t in range(32):
        ss = []
        for s in range(32):
            ridx = np.zeros((128, 128), np.int64)
            cidx = np.zeros((128, 128), np.int64)
            val = np.zeros((128, 128), bool)
            for jr in range(2):
                J = 2 * t + jr
                for ir in range(2):
                    Rq = 2 * s + ir
                    rv = (rstart[Rq] <= J) and (J < rstart[Rq] + 8)
                    sl = (slice(jr * 64, jr * 64 + 64), slice(ir * 64, ir * 64 + 64))
                    if rv:
                        val[sl] = colvalid
                        ridx[sl] = J - Rq + 7
                        cidx[sl] = dcidx
            if not val.any():
                continue
            ss.append(s)
            key = (val.tobytes(), (ridx * val).tobytes(), (cidx * val).tobytes())
            if key not in blocks:
                blocks[key] = len(blk_list)
                blk_list.append((ridx * val, cidx * val, val))
            pair_blk[(t, s)] = blocks[key]
        assert ss == list(range(ss[0], ss[-1] + 1))
        win.append((ss[0], ss[-1]))
    order = []
    tmid = 16
    for s in range(win[tmid][0], win[tmid][1] + 1):
        order.append(pair_blk[(tmid, s)])
    for b in range(len(blk_list)):
        if b not in order:
            order.append(b)
    remap = {old: new for new, old in enumerate(order)}
    blk_list = [blk_list[o] for o in order]
    pair_blk = {k: remap[v] for k, v in pair_blk.items()}
    keyt = []
    for s in range(32):
        ts = [t for t in range(32) if win[t][0] <= s <= win[t][1]]
        assert ts == list(range(ts[0], ts[-1] + 1))
        keyt.append((ts[0], ts[-1]))
    return win, keyt, pair_blk, blk_list


_NA_WIN, _NA_KEYT, _NA_PAIR, _NA_BLKS = _na_plan()
NBLK = len(_NA_BLKS)


def _na_bias_host(rpb):
    L = rpb.shape[0]
    out = np.empty((L, 4, 128, NBLK, 128), np.float32)
    for b, (ri, ci, val) in enumerate(_NA_BLKS):
        g = rpb[:, :, ri, ci]
        out[:, :, :, b, :] = np.where(val[None, None], g, np.float32(NEG))
    return out


def build(depth=DEPTH, debug=False):
    nc = bass.Bass("TRN2", target_bir_lowering=False)
    L = depth

    def din(name, shape, dt=F32):
        return nc.dram_tensor(name, list(shape), dt, kind="ExternalInput").ap()

    x_in = din("x", [S, D])
    w_in = din("w_in", [L, D, INW])
    w_out = din("w_out", [L, D, D])
    w_gate = din("w_gate", [L, D, DFF])
    w_up = din("w_up", [L, D, DFF])
    w_down = din("w_down", [L, DFF, D])
    lru_cw = din("lru_cw", [L, 3, 128, 4])
    lru_vec = din("lru_vec", [L, 3, 128, 7])
    lru_wg = din("lru_wg", [L, 3, 128, 4, 128])
    ret_gnw = din("ret_gnw", [L, 128, 384])
    na_bias = din("na_bias", [L, 4, 128, NBLK, 128])
    ln_gb = din("ln_gb", [L, 128, 4, D])
    c_ident = din("c_ident", [128, 128])
    c_decay = din("c_decay", [128, 6, 128])
    c_wtab = din("c_wtab", [128, 4, 6, 64])
    c_gcol = din("c_gcol", [128, 3])
    c_rope = din("c_rope", [128, 2, 32, 64])
    out = nc.dram_tensor("out", [S, D], F32, kind="ExternalOutput").ap()

    dk = "ExternalOutput" if debug else "Internal"

    def dscr(name, shape, dt, kind="Internal"):
        return nc.dram_tensor(name, list(shape), dt, kind=kind).ap()

    s_wfm = dscr("s_wfm", [L, 10, 128, 8, 128], BF16)
    s_wret = dscr("s_wret", [L, 3, 128, 8, 512], BF16)
    s_wnav = dscr("s_wnav", [L, 128, 8, 256], BF16)
    s_wo = dscr("s_wo", [L, 128, 8, D], BF16)
    s_wg = dscr("s_wg", [L, NF, 128, 8, 128], BF16)
    s_wu = dscr("s_wu", [L, NF, 128, 8, 128], BF16)
    s_wd = dscr("s_wd", [L, 128, NF, D], BF16)
    xT_d = dscr("xT_d", [D, S], BF16)
    yT_d = dscr("yT_d", [D, S], BF16, kind=dk)
    xres_d = dscr("xres_d", [S, D], F32)

    with contextlib.ExitStack() as es:
        kb = KB(nc, es)
        sbt = lambda name, shape, dt: es.enter_context(nc.sbuf_tensor(name, list(shape), dt))
        identf = sbt("identf", [128, 128], F32)
        identb = sbt("identb", [128, 128], BF16)
        c_m05 = sbt("c_m05", [128, 8], F32)
        c_p05 = sbt("c_p05", [128, 512], F32)
        b_const = Buf()
        kb.dma("sp", identf[:], c_ident, writes=[b_const])
        kb.dma("pool", identb[:], c_ident, writes=[b_const])
        kb.op("pool", lambda e: e.memset(c_m05[:], -0.5), writes=[b_const])
        kb.op("pool", lambda e: e.memset(c_p05[:], 0.5), writes=[b_const])

        wb = {k: [Buf() for _ in range(L)] for k in ("fm", "ret", "nav", "wo", "wg", "wu", "wd")}

        def cast_list(l):
            jobs = []
            wv = w_in[l].rearrange("(kc p) n -> p kc n", p=128)
            cols = [0, 128, 256, 384, 512, 640, 2304, 2432, 2560, 2688]
            for t, c0 in enumerate(cols):
                jobs.append(lambda o_i=(s_wfm[l, t], wv[:, :, c0:c0 + 128]), bb=wb["fm"][l]: kb.dma("pool", o_i[0], o_i[1], writes=[bb], accum=True))
            for hp in range(3):
                for gi, g0 in enumerate([768, 1152, 1536, 1920]):
                    c0 = g0 + hp * 128
                    jobs.append(lambda o_i=(s_wret[l, hp, :, :, gi * 128:(gi + 1) * 128], wv[:, :, c0:c0 + 128]), bb=wb["ret"][l]: kb.dma("pool", o_i[0], o_i[1], writes=[bb], accum=True))
            jobs.append(lambda o_i=(s_wnav[l], wv[:, :, 2816:3072]), bb=wb["nav"][l]: kb.dma("pool", o_i[0], o_i[1], writes=[bb], accum=True))
            wov = w_out[l].rearrange("(kc p) n -> p kc n", p=128)
            for kc in range(0, 8, 2):
                jobs.append(lambda o_i=(s_wo[l, :, kc:kc + 2, :], wov[:, kc:kc + 2, :]), bb=wb["wo"][l]: kb.dma("pool", o_i[0], o_i[1], writes=[bb], accum=True))
            wgv = w_gate[l].rearrange("(kc p) n -> p kc n", p=128)
            wuv = w_up[l].rearrange("(kc p) n -> p kc n", p=128)
            for f in range(NF):
                jobs.append(lambda o_i=(s_wg[l, f], wgv[:, :, f * 128:(f + 1) * 128]), bb=wb["wg"][l]: kb.dma("pool", o_i[0], o_i[1], writes=[bb], accum=True))
                jobs.append(lambda o_i=(s_wu[l, f], wuv[:, :, f * 128:(f + 1) * 128]), bb=wb["wu"][l]: kb.dma("pool", o_i[0], o_i[1], writes=[bb], accum=True))
            wdv = w_down[l].rearrange("(fc p) n -> p fc n", p=128)
            for f in range(0, NF, 2):
                jobs.append(lambda o_i=(s_wd[l, :, f:f + 2, :], wdv[:, f:f + 2, :]), bb=wb["wd"][l]: kb.dma("pool", o_i[0], o_i[1], writes=[bb], accum=True))

            return jobs

        for j in cast_list(0):
            j()

        b_xT_d = bufs(NB)
        b_yT_d = [Buf() for _ in range(8)]
        b_xres = bufs(NT)

        with contextlib.ExitStack() as ps:
            sb = lambda name, shape, dt: ps.enter_context(nc.sbuf_tensor(name, list(shape), dt))
            xt = [sb(f"p_xt{i}", [128, D], F32) for i in range(2)]
            xo = [sb(f"p_xo{i}", [128, 8, 512], BF16) for i in range(2)]
            pt = [ps.enter_context(nc.psum_tensor(f"p_pt{i}", [128, 4, 128], F32)) for i in range(2)]
            b_xt, b_xo, b_pt = bufs(2), bufs(2), pbufs(2)
            for tt in range(NT):
                tb, j = divmod(tt, 4)
                xi = tt % 2
                kb.dma("sp", xt[xi][:], x_in[tt * 128:(tt + 1) * 128, :], writes=[b_xt[xi]])
                for hf in range(2):
                    for q in range(4):
                        kc = hf * 4 + q
                        kb.op("pe", lambda e, kc=kc, q=q, hf=hf: e.transpose(pt[hf][:, q, :], xt[xi][:, kc * 128:(kc + 1) * 128], identf[:]),
                              reads=[b_xt[xi], b_const], writes=[b_pt[hf]])
                    eng = "act" if hf == 0 else "dve"
                    if eng == "act":
                        kb.op("act", lambda e, hf=hf: e.copy(out=xo[tb % 2][:, hf * 4:hf * 4 + 4, j * 128:(j + 1) * 128], in_=pt[hf][:]),
                              reads=[b_pt[hf]], writes=[b_xo[tb % 2]])
                    else:
                        kb.op("dve", lambda e, hf=hf: e.tensor_copy(out=xo[tb % 2][:, hf * 4:hf * 4 + 4, j * 128:(j + 1) * 128], in_=pt[hf][:]),
                              reads=[b_pt[hf]], writes=[b_xo[tb % 2]])
                if j == 3:
                    kb.dma("sp", xT_d.rearrange("(kc p) t -> p kc t", p=128)[:, :, tb * 512:(tb + 1) * 512], xo[tb % 2][:],
                           reads=[b_xo[tb % 2]], writes=[b_xT_d[tb]])
        kb.barrier()

        for l in range(L):
            next_casts = cast_list(l + 1) if l + 1 < L else []
            xsrc = x_in if l == 0 else xres_d
            xdst = out if l == L - 1 else xres_d
            with contextlib.ExitStack() as ms:
                xT = ms.enter_context(nc.sbuf_tensor(f"xT_{l}", [128, 8, S], BF16))
                b_xT = bufs(NB)
                xTv = xT_d.rearrange("(kc p) t -> p kc t", p=128)
                for tb in range(NB):
                    kb.dma("sp" if tb % 2 == 0 else "act", xT[:, :, tb * 512:(tb + 1) * 512], xTv[:, :, tb * 512:(tb + 1) * 512],
                           reads=[b_xT_d[tb]], writes=[b_xT[tb]])
                phase_lru(nc, kb, l, xT, b_xT, s_wfm, wb, lru_cw, lru_vec, lru_wg, yT_d, b_yT_d, c_p05, b_const)
                kb.barrier()
                phase_ret(nc, kb, l, xT, b_xT, s_wret, wb, ret_gnw, c_decay, c_wtab, c_gcol, c_rope, yT_d, b_yT_d,
                          identb, c_m05, b_const)
                kb.barrier()
                phase_na(nc, kb, l, xT, b_xT, s_wfm, s_wnav, wb, na_bias, yT_d, b_yT_d, identb, b_const)
                kb.barrier()
            phase_ffn(nc, kb, l, L, s_wo, s_wg, s_wu, s_wd, wb, ln_gb, xsrc, xdst, b_xres, yT_d, b_yT_d, xT_d, b_xT_d,
                      identf, c_m05, b_const, next_casts)
            kb.barrier()
        print("instructions:", kb.ninstr, "semaphores:", kb.nsem + NDS + 1)
    return nc


def phase_lru(nc, kb, l, xT, b_xT, s_wfm, wb, lru_cw, lru_vec, lru_wg, yT_d, b_yT_d, c_p05, b_const):
    with contextlib.ExitStack() as ps:
        sb = lambda name, shape, dt: ps.enter_context(nc.sbuf_tensor(f"{name}_L{l}", list(shape), dt))
        pp = lambda name, shape, dt: ps.enter_context(nc.psum_tensor(f"{name}_L{l}", list(shape), dt))
        LX = sb("l_lx", [128, S + 4], F32)
        XC = sb("l_xc", [128, S], F32)
        XCB = sb("l_xcb", [128, S], BF16)
        GG = sb("l_gg", [128, S], BF16)
        YB = sb("l_yb", [128, S], BF16)
        wx = sb("l_wx", [128, 8, 128], BF16)
        wg_ = sb("l_wg", [128, 8, 128], BF16)
        gm = sb("l_gm", [128, 4, 128], BF16)
        cw = sb("l_cw", [128, 4], F32)
        vec = sb("l_vec", [128, 7], F32)
        der = sb("l_der", [128, 12], F32)
        NSET = 4
        TR = [sb(f"l_tr{i}", [128, 512], F32) for i in range(NSET)]
        TI = [sb(f"l_ti{i}", [128, 512], F32) for i in range(NSET)]
        TA = [sb(f"l_ta{i}", [128, 512], F32) for i in range(NSET)]
        TH = [sb(f"l_th{i}", [128, 512], F32) for i in range(NSET)]
        TQ = [sb(f"l_tq{i}", [128, 512], F32) for i in range(NSET)]
        TU = [sb(f"l_tu{i}", [128, 512], F32) for i in range(NSET)]
        HT = [sb(f"l_ht{i}", [128, 512], F32) for i in range(NSET)]
        P = [pp(f"l_p{i}", [128, 512], F32) for i in range(4)]
        b_P = pbufs(4)
        b_TR, b_TI, b_TA, b_TH, b_TQ, b_TU, b_HT = (bufs(NSET) for _ in range(7))
        b_w = Buf()
        b_sm = Buf()
        b_LX = bufs(NB)
        b_halo = Buf()
        b_XC, b_XCB, b_GG, b_YB = bufs(NB), bufs(NB), bufs(NB), bufs(NB)
        pcnt = 0
        for ct in range(3):
            kb.dma("sp", wx[:], s_wfm[l, ct], reads=[wb["fm"][l]], writes=[b_w])
            kb.dma("act", wg_[:], s_wfm[l, 3 + ct], reads=[wb["fm"][l]], writes=[b_w])
            kb.dma("pool", gm[:], lru_wg[l, ct], writes=[b_w])
            kb.dma("sp", cw[:], lru_cw[l, ct], writes=[b_sm])
            kb.dma("sp", vec[:], lru_vec[l, ct], writes=[b_sm])
            kb.op("act", lambda e: e.activation(out=der[:, 0:2], in_=vec[:, 5:7], func=AF.Exp, scale=-1.0), reads=[b_sm], writes=[b_sm])
            kb.op("act", lambda e: e.activation(out=der[:, 2:4], in_=der[:, 0:2], func=AF.Ln, bias=1.0, scale=1.0), reads=[b_sm], writes=[b_sm])
            kb.op("dve", lambda e: e.tensor_scalar(out=der[:, 4:6], in0=der[:, 2:4], scalar1=-4.0, scalar2=None, op0=ALU.mult), reads=[b_sm], writes=[b_sm])
            kb.op("dve", lambda e: e.tensor_scalar(out=der[:, 6:8], in0=der[:, 2:4], scalar1=4.0, scalar2=None, op0=ALU.mult), reads=[b_sm], writes=[b_sm])
            kb.op("dve", lambda e: e.tensor_scalar(out=der[:, 8:12], in0=vec[:, 1:5], scalar1=0.5, scalar2=None, op0=ALU.mult), reads=[b_sm], writes=[b_sm])
            kb.op("pool", lambda e: e.memset(LX[:, 0:2], 0.0), writes=[b_halo])
            kb.op("pool", lambda e: e.memset(LX[:, S + 2:S + 4], 0.0), writes=[b_halo])
            for blk in range(NB):
                sl = slice(blk * 512, (blk + 1) * 512)
                pi = pcnt % 4; pcnt += 1
                for kc in range(8):
                    kb.op("pe", lambda e, kc=kc, pi=pi: e.matmul(P[pi][:], lhsT=wx[:, kc, :], rhs=xT[:, kc, sl], start=(kc == 0), stop=(kc == 7)),
                          reads=[b_w, b_xT[blk]], writes=[b_P[pi]])
                kb.op("act", lambda e, pi=pi: e.copy(out=LX[:, 2 + blk * 512:2 + (blk + 1) * 512], in_=P[pi][:]), reads=[b_P[pi]], writes=[b_LX[blk]])
                pi = pcnt % 4; pcnt += 1
                for kc in range(8):
                    kb.op("pe", lambda e, kc=kc, pi=pi: e.matmul(P[pi][:], lhsT=wg_[:, kc, :], rhs=xT[:, kc, sl], start=(kc == 0), stop=(kc == 7)),
                          reads=[b_w, b_xT[blk]], writes=[b_P[pi]])
                kb.op("act", lambda e, pi=pi: e.activation(out=GG[:, sl], in_=P[pi][:], func=AF.Gelu), reads=[b_P[pi]], writes=[b_GG[blk]])
            for blk in range(NB):
                sl = slice(blk * 512, (blk + 1) * 512)
                rd = [b_LX[blk], b_LX[min(blk + 1, NB - 1)], b_LX[max(blk - 1, 0)], b_halo, b_sm]
                kb.op("dve", lambda e: e.tensor_scalar(out=XC[:, sl], in0=LX[:, blk * 512:blk * 512 + 512], scalar1=cw[:, 0:1], scalar2=vec[:, 0:1], op0=ALU.mult, op1=ALU.add),
                      reads=rd, writes=[b_XC[blk]])
                for j in range(1, 4):
                    kb.op("dve", lambda e, j=j: e.scalar_tensor_tensor(out=XC[:, sl], in0=LX[:, blk * 512 + j:blk * 512 + j + 512], scalar=cw[:, j:j + 1], in1=XC[:, sl], op0=ALU.mult, op1=ALU.add),
                          reads=rd, writes=[b_XC[blk]])
                kb.op("pool", lambda e: e.tensor_copy(out=XCB[:, sl], in_=XC[:, sl]), reads=[b_XC[blk]], writes=[b_XCB[blk]])
            items = [(0, blk) for blk in range(NB)] + [(1, blk) for blk in range(NB - 1, -1, -1)]

            def s0(i):
                d, blk = items[i]
                sl = slice(blk * 512, (blk + 1) * 512)
                si = i % NSET
                pr = (2 * i) % 4
                pi_ = (2 * i + 1) % 4
                kb.op("pe", lambda e: e.matmul(P[pr][:], lhsT=gm[:, d, :], rhs=XCB[:, sl], start=True, stop=True), reads=[b_w, b_XCB[blk]], writes=[b_P[pr]])
                kb.op("pe", lambda e: e.matmul(P[pi_][:], lhsT=gm[:, 2 + d, :], rhs=XCB[:, sl], start=True, stop=True), reads=[b_w, b_XCB[blk]], writes=[b_P[pi_]])
                kb.op("act", lambda e: e.activation(out=TR[si][:], in_=P[pr][:], func=AF.Tanh, bias=der[:, 8 + d:9 + d], scale=0.5), reads=[b_P[pr], b_sm], writes=[b_TR[si]])
                kb.op("act", lambda e: e.activation(out=TI[si][:], in_=P[pi_][:], func=AF.Tanh, bias=der[:, 10 + d:11 + d], scale=0.5), reads=[b_P[pi_], b_sm], writes=[b_TI[si]])
                kb.op("act", lambda e: e.activation(out=TA[si][:], in_=TR[si][:], func=AF.Exp, bias=der[:, 4 + d:5 + d], scale=der[:, 4 + d:5 + d]), reads=[b_TR[si], b_sm], writes=[b_TA[si]])
                kb.op("act", lambda e: e.activation(out=TH[si][:], in_=TR[si][:], func=AF.Tanh, bias=der[:, 6 + d:7 + d], scale=der[:, 6 + d:7 + d]), reads=[b_TR[si], b_sm], writes=[b_TH[si]])
                kb.op("dve", lambda e: e.tensor_tensor(out=TQ[si][:], in0=TA[si][:], in1=TA[si][:], op=ALU.mult), reads=[b_TA[si]], writes=[b_TQ[si]])
                kb.op("dve", lambda e: e.scalar_tensor_tensor(out=TQ[si][:], in0=TQ[si][:], scalar=1.0, in1=TH[si][:], op0=ALU.add, op1=ALU.mult), reads=[b_TH[si]], writes=[b_TQ[si]])
                kb.op("dve", lambda e: e.scalar_tensor_tensor(out=TU[si][:], in0=TI[si][:], scalar=1.0, in1=XC[:, sl], op0=ALU.add, op1=ALU.mult), reads=[b_TI[si], b_XC[blk]], writes=[b_TU[si]])

            def s1(i):
                d, blk = items[i]
                sl = slice(blk * 512, (blk + 1) * 512)
                si = i % NSET
                first = (i == 0 or i == NB)
                kb.op("dve", lambda e: e.scalar_tensor_tensor(out=TU[si][:], in0=TU[si][:], scalar=0.5, in1=TQ[si][:], op0=ALU.mult, op1=ALU.mult), reads=[b_TQ[si]], writes=[b_TU[si]])
                if d == 0:
                    o_ap = LX[:, 2 + blk * 512:2 + (blk + 1) * 512]
                    init = 0.0 if first else LX[:, 2 + blk * 512 - 1:2 + blk * 512]
                    rd = [b_TA[si], b_TU[si]] + ([] if first else [b_LX[blk - 1]])
                    kb.op("dve", lambda e: e.tensor_tensor_scan(out=o_ap, data0=TA[si][:], data1=TU[si][:], initial=init, op0=ALU.mult, op1=ALU.add),
                          reads=rd, writes=[b_LX[blk]])
                else:
                    sp_ = (i - 1) % NSET
                    init = 0.0 if first else HT[sp_][:, 0:1]
                    rd = [b_TA[si], b_TU[si]] + ([] if first else [b_HT[sp_]])
                    kb.op("dve", lambda e: e.tensor_tensor_scan(out=HT[si][:, ::-1], data0=TA[si][:, ::-1], data1=TU[si][:, ::-1], initial=init, op0=ALU.mult, op1=ALU.add),
                          reads=rd, writes=[b_HT[si]])
                    kb.op("pool", lambda e: e.tensor_tensor(out=TH[si][:], in0=HT[si][:], in1=LX[:, 2 + blk * 512:2 + (blk + 1) * 512], op=ALU.add),
                          reads=[b_HT[si], b_LX[blk]], writes=[b_TH[si]])
                    kb.op("pool", lambda e: e.tensor_tensor(out=YB[:, sl], in0=TH[si][:], in1=GG[:, sl], op=ALU.mult),
                          reads=[b_TH[si], b_GG[blk]], writes=[b_YB[blk]])

            def s1_sqrt(i):
                si = i % NSET
                kb.op("act", lambda e: e.activation(out=TQ[si][:], in_=TQ[si][:], func=AF.Sqrt), writes=[b_TQ[si]])

            npairs = len(items) // 2
            for p in range(npairs + 1):
                if p < npairs:
                    s0(2 * p)
                    s0(2 * p + 1)
                if p >= 1:
                    s1_sqrt(2 * p - 2)
                    s1_sqrt(2 * p - 1)
                    s1(2 * p - 2)
                    s1(2 * p - 1)
            kb.dma("sp", yT_d[ct * 128:(ct + 1) * 128, :], YB[:], reads=b_YB, writes=[b_yT_d[ct]])


def phase_ret(nc, kb, l, xT, b_xT, s_wret, wb, ret_gnw, c_decay, c_wtab, c_gcol, c_rope, yT_d, b_yT_d, identb, c_m05, b_const):
    with contextlib.ExitStack() as ps:
        sb = lambda name, shape, dt: ps.enter_context(nc.sbuf_tensor(f"{name}_L{l}", list(shape), dt))
        pp = lambda name, shape, dt: ps.enter_context(nc.psum_tensor(f"{name}_L{l}", list(shape), dt))
        decay = sb("r_decay", [128, 6, 128], F32)
        wtab = sb("r_wtab", [128, 4, 6, 64], F32)
        gcol = sb("r_gcol", [128, 3], F32)
        rope = sb("r_rope", [128, 2, 32, 64], F32)
        gnw = sb("r_gnw", [128, 384], F32)
        W = sb("r_w", [128, 8, 512], BF16)
        qT = sb("r_qT", [128, S], BF16)
        kT = sb("r_kT", [128, S], BF16)
        VB = sb("r_vb", [128, NT, 128], BF16)
        SG = sb("r_sg", [128, NT, 128], F32)
        STF = sb("r_stf", [128, NT, 64], BF16)
        STB = sb("r_stb", [128, NT, 64], BF16)
        KVB = sb("r_kvb", [128, NT, 64], F32)
        ST32 = sb("r_st32", [128, 64], F32)
        YB = sb("r_yb", [128, S], BF16)
        NSET = 4
        M1 = [sb(f"r_m1{i}", [128, 2, 128], F32) for i in range(NSET)]
        M2 = [sb(f"r_m2{i}", [128, 2, 128], F32) for i in range(NSET)]
        QK = [sb(f"r_qk{i}", [128, 2, 128], BF16) for i in range(NSET)]
        VFB = [sb(f"r_vfb{i}", [128, 2, 2, 64], BF16) for i in range(NSET)]
        STM = [sb(f"r_stm{i}", [128, 2, 128], BF16) for i in range(NSET)]
        OT = [sb(f"r_ot{i}", [128, 2, 64], F32) for i in range(NSET)]
        ASB = [sb(f"r_asb{i}", [128, 2, 64], F32) for i in range(NSET)]
        O2 = [sb(f"r_o2{i}", [128, 2, 64], F32) for i in range(NSET)]
        ST6 = [sb(f"r_st6{i}", [128, 2, 6], F32) for i in range(NSET)]
        MV = [sb(f"r_mv{i}", [128, 2, 2], F32) for i in range(NSET)]
        RS = [sb(f"r_rs{i}", [128, 2], F32) for i in range(NSET)]
        YT = [sb(f"r_yt{i}", [128, 128], BF16) for i in range(NSET)]
        BK = [pp(f"r_bk{i}", [128, 512], F32) for i in range(6)]
        b_BK = pbufs(6)
        PT_t = pp("r_pt", [128, 2, 4, 128], BF16)
        PT = [PT_t[:, 0], PT_t[:, 1]]
        b_ptt = Buf(True)
        b_M1, b_M2, b_QK, b_RS, b_YT = (bufs(NSET) for _ in range(5))
        b_VFB, b_STM, b_OT, b_O2, b_ST6, b_MV, b_ASB = ([bufs(2) for _ in range(NSET)] for _ in range(7))
        b_c = Buf(); b_W = Buf()
        b_qT, b_kT, b_VB, b_SG, b_STF, b_STB, b_KVB, b_YB = (bufs(NT) for _ in range(8))
        b_st = Buf()
        kb.dma("sp", decay[:], c_decay, writes=[b_c], accum=True)
        kb.dma("act", wtab[:], c_wtab, writes=[b_c], accum=True)
        kb.dma("sp", gcol[:], c_gcol, writes=[b_c], accum=True)
        kb.dma("act", rope[:], c_rope, writes=[b_c], accum=True)
        kb.dma("sp", gnw[:], ret_gnw[l], writes=[b_c], accum=True)

        def pipeline(stages, n):
            last = max(sk for sk, _ in stages)
            for i in range(n + last):
                for sk, fn in stages:
                    c = i - sk
                    if 0 <= c < n:
                        fn(c)

        for hp in range(3):
            kb.dma("sp", W[:], s_wret[l, hp], reads=[wb["ret"][l]], writes=[b_W])
            kb.op("pool", lambda e: e.memset(ST32[:], 0.0), writes=[b_st])

            def p1_proj(c):
                cs = slice(c * 128, (c + 1) * 128)
                pa = BK[c % 2]
                for kc in range(8):
                    kb.op("pe", lambda e, kc=kc: e.matmul(pa[:], lhsT=xT[:, kc, cs], rhs=W[:, kc, :], start=(kc == 0), stop=(kc == 7)),
                          reads=[b_xT[c // 4], b_W], writes=[b_BK[c % 2]])

            def p1_elem(c):
                si = c % NSET
                pa = BK[c % 2]; bpa = b_BK[c % 2]
                pv = pa[:, 0:256].rearrange("p (g t f) -> p g t f", g=4, t=2)
                pvs = pv[:, :, ::-1, :]
                ccv = rope[:, 0, c, :].rearrange("p (t f) -> p t f", t=2).unsqueeze(1).broadcast_to([128, 4, 2, 32])
                ssv = rope[:, 1, c, :].rearrange("p (t f) -> p t f", t=2).unsqueeze(1).broadcast_to([128, 4, 2, 32])
                m1v = M1[si][:].rearrange("p a (g t f) -> p (a g) t f", g=2, t=2)
                m2v = M2[si][:].rearrange("p a (g t f) -> p (a g) t f", g=2, t=2)
                kb.op("dve", lambda e: e.tensor_tensor(out=m1v, in0=pv, in1=ccv, op=ALU.mult), reads=[bpa, b_c], writes=[b_M1[si]])
                kb.op("dve", lambda e: e.tensor_tensor(out=m2v, in0=pvs, in1=ssv, op=ALU.mult), reads=[bpa, b_c], writes=[b_M2[si]])
                vv = pa[:, 256:384].rearrange("p (h e) -> p h e", h=2)
                kb.op("dve", lambda e: e.tensor_tensor(out=VFB[si][:, :, 0, :], in0=vv, in1=wtab[:, 0, 2 * hp:2 * hp + 2, :], op=ALU.mult), reads=[bpa, b_c], writes=[b_VFB[si][0]])
                kb.op("dve", lambda e: e.tensor_tensor(out=VFB[si][:, :, 1, :], in0=vv, in1=wtab[:, 1, 2 * hp:2 * hp + 2, :], op=ALU.mult), reads=[bpa, b_c], writes=[b_VFB[si][1]])
                kb.op("pool", lambda e: e.tensor_tensor(out=QK[si][:], in0=M1[si][:], in1=M2[si][:], op=ALU.add), reads=[b_M1[si], b_M2[si]], writes=[b_QK[si]])
                kb.op("act", lambda e: e.copy(out=VB[:, c, :], in_=pa[:, 256:384]), reads=[bpa], writes=[b_VB[c]])
                kb.op("act", lambda e: e.activation(out=SG[:, c, :], in_=pa[:, 384:512], func=AF.Silu), reads=[bpa], writes=[b_SG[c]])
                kb.op("pool", lambda e: e.tensor_tensor(out=SG[:, c, :], in0=SG[:, c, :], in1=gnw[:, hp * 128:(hp + 1) * 128], op=ALU.mult), reads=[b_c], writes=[b_SG[c]])

            def p1_pe2(c):
                si = c % NSET
                cs = slice(c * 128, (c + 1) * 128)
                ptt = PT[c % 2]
                for a in range(2):
                    kb.op("pe", lambda e, a=a: e.transpose(ptt[:, a, :], QK[si][:, a, :], identb[:]), reads=[b_QK[si], b_const], writes=[b_ptt])
                kb.op("act", lambda e: e.copy(out=qT[:, cs], in_=ptt[:, 0, :]), reads=[b_ptt], writes=[b_qT[c]])
                kb.op("act", lambda e: e.copy(out=kT[:, cs], in_=ptt[:, 1, :]), reads=[b_ptt], writes=[b_kT[c]])
                pk = BK[2 + c % 2][:, 0:128].rearrange("p (a e) -> p a e", a=2)
                for h in range(2):
                    kb.op("pe", lambda e, h=h: e.matmul(pk[h * 64:(h + 1) * 64, :, :], lhsT=QK[si][:, 1, h * 64:(h + 1) * 64], rhs=VFB[si][:, h, :, :], start=True, stop=True),
                          reads=[b_QK[si]] + b_VFB[si], writes=[b_BK[2 + c % 2]])

            def p1_state(c):
                pk = BK[2 + c % 2][:, 0:128].rearrange("p (a e) -> p a e", a=2)
                kb.op("pool", lambda e: e.tensor_copy(out=STF[:, c, :], in_=ST32[:]), reads=[b_st], writes=[b_STF[c]])
                kb.op("dve", lambda e: e.scalar_tensor_tensor(out=ST32[:], in0=ST32[:], scalar=gcol[:, hp:hp + 1], in1=pk[:, 0, :], op0=ALU.mult, op1=ALU.add),
                      reads=[b_BK[2 + c % 2], b_c, b_STF[c]], writes=[b_st])
                kb.op("dve", lambda e: e.tensor_copy(out=KVB[:, c, :], in_=pk[:, 1, :]), reads=[b_BK[2 + c % 2]], writes=[b_KVB[c]])

            pipeline([(0, p1_proj), (1, p1_elem), (1, p1_pe2), (2, p1_state)], NT)

            kb.op("pool", lambda e: e.memset(ST32[:], 0.0), reads=[b_STF[NT - 1]], writes=[b_st])
            for c in range(NT - 1, -1, -1):
                kb.op("pool", lambda e, c=c: e.tensor_copy(out=STB[:, c, :], in_=ST32[:]), reads=[b_st], writes=[b_STB[c]])
                kb.op("dve", lambda e, c=c: e.scalar_tensor_tensor(out=ST32[:], in0=ST32[:], scalar=gcol[:, hp:hp + 1], in1=KVB[:, c, :], op0=ALU.mult, op1=ALU.add),
                      reads=[b_KVB[c], b_c, b_STB[c]], writes=[b_st])

            def cc(i):
                return NT - 1 - i

            def p2_scores(i):
                c = cc(i); cs = slice(c * 128, (c + 1) * 128); si = i % NSET
                for h in range(2):
                    hs = slice(h * 64, (h + 1) * 64)
                    kb.op("pe", lambda e, h=h, hs=hs: e.matmul(BK[h][:, 0:128], lhsT=kT[hs, cs], rhs=qT[hs, cs], start=True, stop=True),
                          reads=[b_kT[c], b_qT[c]], writes=[b_BK[h]])
                for h in range(2):
                    kb.op("dve", lambda e, h=h: e.tensor_tensor(out=STM[si][:, h, :], in0=BK[h][:, 0:128], in1=decay[:, 2 * hp + h, :], op=ALU.mult),
                          reads=[b_BK[h], b_c], writes=[b_STM[si][h]])

            def p2_out(i):
                c = cc(i); cs = slice(c * 128, (c + 1) * 128); si = i % NSET
                st = 2 + 2 * (i % 2)
                for h in range(2):
                    hs = slice(h * 64, (h + 1) * 64)
                    po = BK[st + h]
                    kb.op("pe", lambda e, h=h, hs=hs, po=po: e.matmul(po[:, 0:64], lhsT=STM[si][:, h, :], rhs=VB[:, c, hs], start=True, stop=True),
                          reads=[b_STM[si][h], b_VB[c]], writes=[b_BK[st + h]])
                    kb.op("pe", lambda e, h=h, hs=hs, po=po: e.matmul(po[:, 64:128], lhsT=qT[hs, cs], rhs=STF[hs, c, :], start=True, stop=True),
                          reads=[b_qT[c], b_STF[c]], writes=[b_BK[st + h]])
                    kb.op("pe", lambda e, h=h, hs=hs, po=po: e.matmul(po[:, 128:192], lhsT=qT[hs, cs], rhs=STB[hs, c, :], start=True, stop=True),
                          reads=[b_qT[c], b_STB[c]], writes=[b_BK[st + h]])
                for h in range(2):
                    po = BK[st + h]
                    kb.op("act", lambda e, h=h, po=po: e.copy(out=ASB[si][:, h, :], in_=po[:, 0:64]), reads=[b_BK[st + h]], writes=[b_ASB[si][h]])
                for h in range(2):
                    hg = 2 * hp + h
                    po = BK[st + h]
                    kb.op("dve", lambda e, h=h, hg=hg, po=po: e.scalar_tensor_tensor(out=O2[si][:, h, :], in0=po[:, 128:192], scalar=wtab[:, 3, hg, 0:1], in1=ASB[si][:, h, :], op0=ALU.mult, op1=ALU.add),
                          reads=[b_BK[st + h], b_c, b_ASB[si][h]], writes=[b_O2[si][h]])
                for h in range(2):
                    hg = 2 * hp + h
                    po = BK[st + h]
                    kb.op("dve", lambda e, h=h, hg=hg, po=po: e.scalar_tensor_tensor(out=OT[si][:, h, :], in0=po[:, 64:128], scalar=wtab[:, 2, hg, 0:1], in1=O2[si][:, h, :], op0=ALU.mult, op1=ALU.add),
                          reads=[b_BK[st + h], b_c, b_O2[si][h]], writes=[b_OT[si][h]])
                for h in range(2):
                    kb.op("dve", lambda e, h=h: e.bn_stats(out=ST6[si][:, h, :], in_=OT[si][:, h, :]), reads=[b_OT[si][h]], writes=[b_ST6[si][h]])
                for h in range(2):
                    kb.op("dve", lambda e, h=h: e.bn_aggr(out=MV[si][:, h, :], in_=ST6[si][:, h, :]), reads=[b_ST6[si][h]], writes=[b_MV[si][h]])
                kb.op("pool", lambda e: e.tensor_scalar(out=RS[si][:], in0=MV[si][:, :, 1], scalar1=GN_EPS, scalar2=None, op0=ALU.add), reads=b_MV[si], writes=[b_RS[si]])
                kb.op("pool", lambda e: e.tensor_tensor(out=RS[si][:], in0=RS[si][:], in1=c_m05[:, 0:2], op=ALU.pow), reads=[b_const], writes=[b_RS[si]])

            def p2_norm(i):
                c = cc(i); si = i % NSET
                for h in range(2):
                    kb.op("dve", lambda e, h=h: e.tensor_scalar(out=OT[si][:, h, :], in0=OT[si][:, h, :], scalar1=MV[si][:, h, 0:1], scalar2=RS[si][:, h:h + 1], op0=ALU.subtract, op1=ALU.mult),
                          reads=[b_MV[si][h], b_RS[si]], writes=[b_OT[si][h]])
                kb.op("pool", lambda e: e.tensor_tensor(out=YT[si][:], in0=OT[si][:].rearrange("p h e -> p (h e)"), in1=SG[:, c, :], op=ALU.mult), reads=b_OT[si] + [b_SG[c]], writes=[b_YT[si]])

            def p2_tr(i):
                c = cc(i); cs = slice(c * 128, (c + 1) * 128); si = i % NSET
                ptt = PT[i % 2]
                kb.op("pe", lambda e: e.transpose(ptt[:, 0, :], YT[si][:], identb[:]), reads=[b_YT[si], b_const], writes=[b_ptt])
                kb.op("act", lambda e: e.copy(out=YB[:, cs], in_=ptt[:, 0, :]), reads=[b_ptt], writes=[b_YB[c]])

            pipeline([(0, p2_scores), (1, p2_out), (2, p2_norm), (3, p2_tr)], NT)
            kb.dma("sp", yT_d[384 + hp * 128:384 + (hp + 1) * 128, :], YB[:], reads=b_YB, writes=[b_yT_d[3 + hp]])


def phase_na(nc, kb, l, xT, b_xT, s_wfm, s_wnav, wb, na_bias, yT_d, b_yT_d, identb, b_const):
    with contextlib.ExitStack() as ps:
        sb = lambda name, shape, dt: ps.enter_context(nc.sbuf_tensor(f"{name}_L{l}", list(shape), dt))
        pp = lambda name, shape, dt: ps.enter_context(nc.psum_tensor(f"{name}_L{l}", list(shape), dt))
        QKt = [sb(f"n_qk{i}", [128, S], BF16) for i in range(4)]
        VA = sb("n_va", [128, NT, 4, 65], BF16)
        BIAS = sb("n_bias", [128, NBLK, 128], F32)
        Wt = [sb(f"n_w{i}", [128, 8, 128], BF16) for i in range(2)]
        Wv = sb("n_wv", [128, 8, 256], BF16)
        NR = 7
        PTr = [sb(f"n_pt{i}", [128, 768], BF16) for i in range(NR)]
        EP = [sb(f"n_ep{i}", [128, 768], F32) for i in range(2)]
        YTK = sb("n_ytk", [128, NT, 256], BF16)
        RC = [sb(f"n_rc{i}", [128, 1], F32) for i in range(2)]
        PTb_t = pp("n_ptb", [128, 2, 4, 128], BF16)
        PTb = [PTb_t[:, 0], PTb_t[:, 1]]
        PS_ = [pp(f"n_ps{i}", [128, 1024], F32) for i in range(2)]
        PO_t = pp("n_po", [128, 512], F32)
        PO = [PO_t[:, 0:128], PO_t[:, 128:256]]
        PX = [pp(f"n_px{i}", [128, 512], F32) for i in range(2)]

        b_QK = [bufs(NB) for _ in range(4)]
        b_VA = bufs(NT); b_va1 = Buf()
        b_BIAS = Buf(); b_Wt = bufs(2); b_Wv = Buf()
        b_PT = bufs(NR); b_EP = bufs(2); b_YTK = bufs(NT); b_RC = bufs(2)
        b_PS, b_PX = pbufs(2), pbufs(2)
        b_po1 = Buf(True)
        b_PO = [b_po1, b_po1]
        kb.op("pool", lambda e: e.memset(VA[:], 1.0), writes=[b_va1])
        kb.dma("act", Wv[:], s_wnav[l], reads=[wb["nav"][l]], writes=[b_Wv])
        pcnt = 0
        for t4 in range(4):
            wi = t4 % 2
            kb.dma("sp", Wt[wi][:], s_wfm[l, 6 + t4], reads=[wb["fm"][l]], writes=[b_Wt[wi]])
            for blk in range(NB):
                sl = slice(blk * 512, (blk + 1) * 512)
                pi = pcnt % 2; pcnt += 1
                for kc in range(8):
                    kb.op("pe", lambda e, kc=kc: e.matmul(PX[pi][:], lhsT=Wt[wi][:, kc, :], rhs=xT[:, kc, sl], start=(kc == 0), stop=(kc == 7)),
                          reads=[b_Wt[wi], b_xT[blk]], writes=[b_PX[pi]])
                if t4 < 2:
                    kb.op("act", lambda e: e.mul(out=QKt[t4][:, sl], in_=PX[pi][:], mul=0.125), reads=[b_PX[pi]], writes=[b_QK[t4][blk]])
                else:
                    kb.op("dve", lambda e: e.tensor_copy(out=QKt[t4][:, sl], in_=PX[pi][:]), reads=[b_PX[pi]], writes=[b_QK[t4][blk]])
        for tt in range(NT):
            ts_ = slice(tt * 128, (tt + 1) * 128)
            pi = pcnt % 2; pcnt += 1
            for kc in range(8):
                kb.op("pe", lambda e, kc=kc: e.matmul(PX[pi][:, 0:256], lhsT=xT[:, kc, ts_], rhs=Wv[:, kc, :], start=(kc == 0), stop=(kc == 7)),
                      reads=[b_Wv, b_xT[tt // 4]], writes=[b_PX[pi]])
            eng = "act" if tt % 2 == 0 else "dve"
            src = PX[pi][:, 0:256].rearrange("p (h e) -> p h e", h=4)
            if eng == "act":
                kb.op("act", lambda e: e.copy(out=VA[:, tt, :, 0:64], in_=src), reads=[b_PX[pi], b_va1], writes=[b_VA[tt]])
            else:
                kb.op("dve", lambda e: e.tensor_copy(out=VA[:, tt, :, 0:64], in_=src), reads=[b_PX[pi], b_va1], writes=[b_VA[tt]])
        import os
        if os.environ.get("NA_STOP") == "1":
            return
        it = 0
        pending = []
        pending2 = []
        if os.environ.get("NA_STOP") == "3":
            kb.op("pool", lambda e: e.memset(YTK[:], 1.0), writes=b_YTK)
        for h in range(4):
            if os.environ.get("NA_STOP") == "3":
                break
            if os.environ.get("NA_STOP") == "2" and h > 0:
                break
            kb.dma("sp", BIAS[:], na_bias[l, h], writes=[b_BIAS])
            qt = QKt[h // 2]; kt = QKt[2 + h // 2]
            bq = b_QK[h // 2]; bk = b_QK[2 + h // 2]
            hs = slice((h % 2) * 64, (h % 2) * 64 + 64)
            for t in range(NT):
                if os.environ.get("NA_STOP") == "2" and t > 3:
                    break
                s_lo, s_hi = _NA_WIN[t]
                nq = s_hi - s_lo + 1
                N = nq * 128
                pi = it % 2; ri = it % NR; it += 1
                pS = PS_[pi]
                q0 = s_lo * 128
                rdq = [bq[b] for b in range(q0 // 512, (q0 + N - 1) // 512 + 1)]
                for c0 in range(0, N, 512):
                    n = min(512, N - c0)
                    kb.op("pe", lambda e, c0=c0, n=n: e.matmul(pS[:, c0:c0 + n], lhsT=kt[hs, t * 128:(t + 1) * 128], rhs=qt[hs, q0 + c0:q0 + c0 + n], start=True, stop=True),
                          reads=[bk[t // 4]] + rdq, writes=[b_PS[pi]])
                s = s_lo
                while s <= s_hi:
                    b0 = _NA_PAIR[(t, s)]
                    e_ = s
                    while e_ + 1 <= s_hi and _NA_PAIR[(t, e_ + 1)] == b0 + (e_ + 1 - s):
                        e_ += 1
                    n = e_ - s + 1
                    o0 = (s - s_lo) * 128
                    kb.op("dve", lambda e, o0=o0, n=n, b0=b0: e.tensor_tensor(out=EP[pi][:, o0:o0 + n * 128], in0=pS[:, o0:o0 + n * 128],
                                                                            in1=BIAS[:, b0:b0 + n, :].rearrange("p b q -> p (b q)"), op=ALU.add),
                          reads=[b_PS[pi], b_BIAS], writes=[b_EP[pi]])
                    s = e_ + 1
                kb.op("act", lambda e: e.activation(out=PTr[ri][:, 0:N], in_=EP[pi][:, 0:N], func=AF.Exp), reads=[b_EP[pi]], writes=[b_PT[ri]])
                ring_of_t = {tt_: (it - 1 - (t - tt_)) % NR for tt_ in range(max(0, t - NR + 1), t + 1)}
                def pv(t=t, h=h, ring_of_t=ring_of_t):
                    for sq in range(NT):
                        t_lo, t_hi = _NA_KEYT[sq]
                        if t_hi != t or os.environ.get("NA_NOPV") == "1":
                            continue
                        assert t - t_lo < NR - 1
                        po = PO[sq % 2]; bpo = b_PO[sq % 2]
                        for tk in range(t_lo, t_hi + 1):
                            rk = ring_of_t[tk]
                            off = (sq - _NA_WIN[tk][0]) * 128
                            kb.op("pe", lambda e, tk=tk, rk=rk, off=off: e.matmul(po[:, 0:65], lhsT=PTr[rk][:, off:off + 128], rhs=VA[:, tk, h, :], start=(tk == t_lo), stop=(tk == t_hi)),
                                  reads=[b_PT[rk], b_VA[tk]], writes=[bpo])
                        rc = RC[sq % 2]
                        kb.op("dve", lambda e: e.reciprocal(out=rc[:], in_=po[:, 64:65]), reads=[bpo], writes=[b_RC[sq % 2]])
                        kb.op("dve", lambda e: e.tensor_scalar(out=YTK[:, sq, h * 64:(h + 1) * 64], in0=po[:, 0:64], scalar1=rc[:, 0:1], scalar2=None, op0=ALU.mult),
                              reads=[bpo, b_RC[sq % 2]], writes=[b_YTK[sq]])
                for f in pending:
                    f()
                pending = [pv]
            for f in pending:
                f()
            pending = []
        b_ptb1 = Buf(True)
        b_PTb = [b_ptb1, b_ptb1]
        for sq in range(NT):
            pi = sq % 2
            for a in range(2):
                kb.op("pe", lambda e, a=a: e.transpose(PTb[pi][:, a, :], YTK[:, sq, a * 128:(a + 1) * 128], identb[:]), reads=[b_YTK[sq], b_const], writes=[b_PTb[pi]])
            for a in range(2):
                eng = "dve"
                if eng == "act":
                    kb.op("act", lambda e, a=a: e.copy(out=QKt[a][:, sq * 128:(sq + 1) * 128], in_=PTb[pi][:, a, :]), reads=[b_PTb[pi]], writes=[b_QK[a][sq // 4]])
                else:
                    kb.op("dve", lambda e, a=a: e.tensor_copy(out=QKt[a][:, sq * 128:(sq + 1) * 128], in_=PTb[pi][:, a, :]), reads=[b_PTb[pi]], writes=[b_QK[a][sq // 4]])
        for a in range(2):
            kb.dma("sp", yT_d[768 + a * 128:768 + (a + 1) * 128, :], QKt[a][:], reads=b_QK[a], writes=[b_yT_d[6 + a]])


def phase_ffn(nc, kb, l, L, s_wo, s_wg, s_wu, s_wd, wb, ln_gb, xsrc, xdst, b_xres, yT_d, b_yT_d, xT_d, b_xT_d, identf, c_m05, b_const, next_casts=()):
    last = (l == L - 1)
    with contextlib.ExitStack() as ps:
        sb = lambda name, shape, dt: ps.enter_context(nc.sbuf_tensor(f"{name}_L{l}", list(shape), dt))
        pp = lambda name, shape, dt: ps.enter_context(nc.psum_tensor(f"{name}_L{l}", list(shape), dt))
        GB = sb("f_gb", [128, 4, D], F32)
        WO = sb("f_wo", [128, 8, D], BF16)
        WD = sb("f_wd", [128, NF, D], BF16)
        YT = sb("f_yt", [128, 8, 512], BF16)
        XR = [sb(f"f_xr{i}", [128, D], F32) for i in range(2)]
        X1 = sb("f_x1", [128, 4, D], F32)
        X1T = sb("f_x1t", [128, 8, 512], BF16)
        HT = sb("f_ht", [128, NF, 512], BF16)
        NW = 3
        WG = [sb(f"f_wg{i}", [128, 8, 128], BF16) for i in range(NW)]
        WU = [sb(f"f_wu{i}", [128, 8, 128], BF16) for i in range(NW)]
        RB = [sb(f"f_rb{i}", [128, D], F32) for i in range(2)]
        SGT = [sb(f"f_sg{i}", [128, 512], F32) for i in range(2)]
        XO = [sb(f"f_xo{i}", [128, D], F32) for i in range(2)]
        XOT = sb("f_xot", [128, 8, 512], BF16)
        ST6 = [sb(f"f_st6{i}", [128, 2, 6], F32) for i in range(2)]
        MV = [sb(f"f_mv{i}", [128, 2], F32) for i in range(2)]
        RS = [sb(f"f_rs{i}", [128, 2], F32) for i in range(2)]
        PA = [pp(f"f_pa{i}", [128, 2, 512], F32) for i in range(2)]
        PTt = pp("f_pt", [128, 2, 512], F32)
        PG = pp("f_pg", [128, 512], F32)
        PU = pp("f_pu", [128, 512], F32)
        b_GB, b_WO, b_WD, b_YT = Buf(), Buf(), Buf(), Buf()
        b_XR = bufs(2); b_X1 = bufs(4); b_X1T = Buf(); b_HT = bufs(NF)
        b_WG, b_WU = bufs(NW), bufs(NW)
        b_RB, b_XO, b_ST6, b_MV, b_RS = bufs(2), bufs(2), bufs(2), bufs(2), bufs(2)
        b_XOT = Buf()
        b_SGT = bufs(2)
        b_PA = pbufs(2)
        b_PTt, b_PG, b_PU = Buf(True), Buf(True), Buf(True)
        kb.dma("sp", GB[:], ln_gb[l], writes=[b_GB])
        kb.dma("act", WO[:], s_wo[l], reads=[wb["wo"][l]], writes=[b_WO])
        kb.dma("act", WD[:], s_wd[l], reads=[wb["wd"][l]], writes=[b_WD])
        yv = yT_d.rearrange("(c p) t -> p c t", p=128)
        xTv = xT_d.rearrange("(kc p) t -> p kc t", p=128)
        fcnt = 0
        lncnt = 0

        def ln_chain(pa_i, resid_ap, resid_bufs, gi, out_ap, out_bufs):
            nonlocal lncnt
            k = lncnt % 2; lncnt += 1
            rb = RB[k]
            kb.op("dve", lambda e: e.scalar_tensor_tensor(out=rb[:], in0=resid_ap, scalar=ALPHA, in1=PA[pa_i][:].rearrange("p a n -> p (a n)"), op0=ALU.mult, op1=ALU.add),
                  reads=resid_bufs + [b_PA[pa_i]], writes=[b_RB[k]])
            for hf in range(2):
                kb.op("dve", lambda e, hf=hf: e.bn_stats(out=ST6[k][:, hf, :], in_=rb[:, hf * 512:(hf + 1) * 512]), reads=[b_RB[k]], writes=[b_ST6[k]])
            kb.op("dve", lambda e: e.bn_aggr(out=MV[k][:], in_=ST6[k][:].rearrange("p a s -> p (a s)")), reads=[b_ST6[k]], writes=[b_MV[k]])
            kb.op("pool", lambda e: e.tensor_scalar(out=RS[k][:, 0:1], in0=MV[k][:, 1:2], scalar1=LN_EPS, scalar2=None, op0=ALU.add), reads=[b_MV[k]], writes=[b_RS[k]])
            kb.op("pool", lambda e: e.tensor_tensor(out=RS[k][:, 0:1], in0=RS[k][:, 0:1], in1=c_m05[:, 0:1], op=ALU.pow), reads=[b_const], writes=[b_RS[k]])
            kb.op("pool", lambda e: e.tensor_tensor(out=RS[k][:, 1:2], in0=MV[k][:, 0:1], in1=RS[k][:, 0:1], op=ALU.mult), reads=[b_MV[k]], writes=[b_RS[k]])
            kb.op("pool", lambda e: e.tensor_scalar(out=RS[k][:, 1:2], in0=RS[k][:, 1:2], scalar1=-1.0, scalar2=None, op0=ALU.mult), writes=[b_RS[k]])
            kb.op("act", lambda e: e.activation(out=rb[:], in_=rb[:], func=AF.Identity, bias=RS[k][:, 1:2], scale=RS[k][:, 0:1]), reads=[b_RS[k]], writes=[b_RB[k]])
            kb.op("dve", lambda e: e.tensor_tensor(out=rb[:], in0=rb[:], in1=GB[:, gi, :], op=ALU.mult), reads=[b_GB], writes=[b_RB[k]])
            kb.op("pool", lambda e: e.tensor_tensor(out=out_ap, in0=rb[:], in1=GB[:, gi + 1, :], op=ALU.add), reads=[b_GB, b_RB[k]], writes=out_bufs)

        def transposes(src_ap, src_bufs, tdst, tdst_bufs, j):
            for hf in range(2):
                for q in range(4):
                    kc = hf * 4 + q
                    kb.op("pe", lambda e, kc=kc, q=q, hf=hf: e.transpose(PTt[:, hf, q * 128:(q + 1) * 128], src_ap[:, kc * 128:(kc + 1) * 128], identf[:]),
                          reads=src_bufs + [b_const], writes=[b_PTt])
            kb.op("act", lambda e: e.copy(out=tdst[:, :, j * 128:(j + 1) * 128], in_=PTt[:].rearrange("p a (q n) -> p (a q) n", q=4)),
                  reads=[b_PTt], writes=tdst_bufs)

        def load_yt(tb):
            kb.dma("sp", YT[:], yv[:, :, tb * 512:(tb + 1) * 512], reads=b_yT_d, writes=[b_YT])

        def stage_a(tb, j):
            tt = tb * 4 + j
            xi = tt % 2
            kb.dma("act", XR[xi][:], xsrc[tt * 128:(tt + 1) * 128, :], reads=[b_xres[tt]], writes=[b_XR[xi]])
            for hf in range(2):
                for c in range(8):
                    kb.op("pe", lambda e, c=c, hf=hf: e.matmul(PA[1][:, hf, :], lhsT=YT[:, c, j * 128:(j + 1) * 128], rhs=WO[:, c, hf * 512:(hf + 1) * 512], start=(c == 0), stop=(c == 7)),
                          reads=[b_YT, b_WO], writes=[b_PA[1]])
            ln_chain(1, XR[xi][:], [b_XR[xi]], 0, X1[:, j, :], [b_X1[j]])

        def t1(j):
            transposes(X1[:, j, :], [b_X1[j]], X1T, [b_X1T], j)

        def stage_c(tb, j):
            tt = tb * 4 + j
            for hf in range(2):
                for f in range(NF):
                    kb.op("pe", lambda e, f=f, hf=hf: e.matmul(PA[0][:, hf, :], lhsT=HT[:, f, j * 128:(j + 1) * 128], rhs=WD[:, f, hf * 512:(hf + 1) * 512], start=(f == 0), stop=(f == NF - 1)),
                          reads=[b_HT[f], b_WD], writes=[b_PA[0]])
            ln_chain(0, X1[:, j, :], [b_X1[j]], 2, XO[j % 2][:], [b_XO[j % 2]])
            kb.dma("sp", xdst[tt * 128:(tt + 1) * 128, :], XO[j % 2][:], reads=[b_XO[j % 2]], writes=[b_xres[tt]])

        def t2(j):
            if not last:
                transposes(XO[j % 2][:], [b_XO[j % 2]], XOT, [b_XOT], j)

        load_yt(0)
        for j in range(4):
            stage_a(0, j)
            if j >= 1:
                t1(j - 1)
        t1(3)
        for tb in range(NB):
            for jb in list(next_casts)[tb::NB]:
                jb()
            nxt = tb + 1 < NB
            if nxt:
                load_yt(tb + 1)
            for f in range(NF):
                wi = fcnt % NW; pi = fcnt % 2; fcnt += 1
                kb.dma("sp", WG[wi][:], s_wg[l, f], reads=[wb["wg"][l]], writes=[b_WG[wi]])
                kb.dma("act", WU[wi][:], s_wu[l, f], reads=[wb["wu"][l]], writes=[b_WU[wi]])
                for kc in range(8):
                    kb.op("pe", lambda e, kc=kc: e.matmul(PG[:], lhsT=WG[wi][:, kc, :], rhs=X1T[:, kc, :], start=(kc == 0), stop=(kc == 7)),
                          reads=[b_WG[wi], b_X1T], writes=[b_PG])
                for kc in range(8):
                    kb.op("pe", lambda e, kc=kc: e.matmul(PU[:], lhsT=WU[wi][:, kc, :], rhs=X1T[:, kc, :], start=(kc == 0), stop=(kc == 7)),
                          reads=[b_WU[wi], b_X1T], writes=[b_PU])
                kb.op("act", lambda e: e.activation(out=SGT[pi][:], in_=PG[:], func=AF.Silu), reads=[b_PG], writes=[b_SGT[pi]])
                kb.op("dve", lambda e: e.tensor_tensor(out=HT[:, f, :], in0=SGT[pi][:], in1=PU[:], op=ALU.mult), reads=[b_SGT[pi], b_PU], writes=[b_HT[f]])
            for j in range(4):
                stage_c(tb, j)
                if nxt:
                    stage_a(tb + 1, j)
                if j >= 1:
                    t2(j - 1)
                    if nxt:
                        t1(j - 1)
            t2(3)
            if nxt:
                t1(3)
            if not last:
                kb.dma("sp", xTv[:, :, tb * 512:(tb + 1) * 512], XOT[:], reads=[b_XOT], writes=[b_xT_d[tb]])


_NC_CACHE = {}


def _host_inputs(inputs, depth=DEPTH):
    f = lambda a: np.ascontiguousarray(np.asarray(a, dtype=np.float32))
    L = depth
    conv_w = f(inputs["conv_w"])[:L]
    lru_cw = np.ascontiguousarray(conv_w.reshape(L, 4, 3, 128).transpose(0, 2, 3, 1))
    cols = [f(inputs["conv_b"])[:L], f(inputs["lru_b_a"])[:L, 0], f(inputs["lru_b_a"])[:L, 1], f(inputs["lru_b_x"])[:L, 0],
            f(inputs["lru_b_x"])[:L, 1], f(inputs["lru_lam"])[:L, 0], f(inputs["lru_lam"])[:L, 1]]
    lru_vec = np.ascontiguousarray(np.stack([c.reshape(L, 3, 128) for c in cols], axis=-1))
    wa = f(inputs["lru_w_a"])[:L]; wx = f(inputs["lru_w_x"])[:L]
    lru_wg = np.zeros((L, 3, 128, 4, 128), np.float32)
    for ct in range(3):
        for g in range(2):
            sl = slice(g * 64, (g + 1) * 64)
            for d in range(2):
                lru_wg[:, ct, sl, d, sl] = wa[:, d, 2 * ct + g]
                lru_wg[:, ct, sl, 2 + d, sl] = wx[:, d, 2 * ct + g]
    ret_gnw = np.ascontiguousarray(np.broadcast_to(f(inputs["ret_gn_w"])[:L, None, :], (L, 128, 384)))
    na_bias = _na_bias_host(f(inputs["na_rpb"])[:L])
    gb = np.stack([f(inputs["ln1_g"])[:L], f(inputs["ln1_b"])[:L], f(inputs["ln2_g"])[:L], f(inputs["ln2_b"])[:L]], axis=1)
    ln_gb = np.ascontiguousarray(np.broadcast_to(gb[:, None], (L, 128, 4, D)))
    decayT, wtab, gcol, rope = _ret_consts()
    common = {
        "w_in": f(inputs["w_in"])[:L], "w_out": f(inputs["w_out"])[:L], "w_gate": f(inputs["w_gate"])[:L],
        "w_up": f(inputs["w_up"])[:L], "w_down": f(inputs["w_down"])[:L],
        "lru_cw": lru_cw, "lru_vec": lru_vec, "lru_wg": lru_wg, "ret_gnw": ret_gnw, "na_bias": na_bias, "ln_gb": ln_gb,
        "c_ident": np.eye(128, dtype=np.float32), "c_decay": decayT, "c_wtab": wtab, "c_gcol": gcol, "c_rope": rope,
    }
    return common


def kernel(**inputs):
    x = np.asarray(inputs["x"], dtype=np.float32)
    B = x.shape[0]
    common = _host_inputs(inputs)
    if "nc" not in _NC_CACHE:
        _NC_CACHE["nc"] = build(DEPTH)
    nc = _NC_CACHE["nc"]
    in_maps = []
    for b in range(B):
        m = dict(common)
        m["x"] = np.ascontiguousarray(x[b])
        in_maps.append(m)
    res = run_bass_kernel_spmd(nc, in_maps, core_ids=list(range(B)))
    return np.stack([np.asarray(r["out"], dtype=np.float32) for r in res.results], axis=0)
```

```python
import contextlib
import numpy as np
import concourse.bass as bass
import concourse.mybir as mybir
from concourse.bass_utils import run_bass_kernel_spmd

F32 = mybir.dt.float32
BF16 = mybir.dt.bfloat16
AF = mybir.ActivationFunctionType
ALU = mybir.AluOpType

S = 4096
D = 1024
NT = 32
NB = 8
INW = 3072
DFF = 2816
NF = 22
DEPTH = 4
ALPHA = float((2 * DEPTH) ** 0.25)
LN_EPS = 1e-5
GN_EPS = 1e-6
NEG = -30000.0
NDS = 48
NHW = 24


class Buf:
    __slots__ = ("w", "r", "excl")

    def __init__(self, excl=False):
        self.w = []
        self.r = {}
        self.excl = excl


def bufs(n):
    return [Buf() for _ in range(n)]


def pbufs(n):
    return [Buf(True) for _ in range(n)]


class KB:
    def __init__(self, nc, es):
        self.nc = nc
        self.es = es
        self.E = {"pe": nc.tensor, "act": nc.scalar, "dve": nc.vector, "pool": nc.gpsimd, "sp": nc.sync}
        self.epoch = 0
        self.nsem = 10
        self.semsets = [{n: es.enter_context(nc.semaphore(f"e{k}_{n}")) for n in self.E} for k in range(2)]
        self.sem = dict(self.semsets[0])
        self.cnt = {n: 0 for n in self.E}
        self.seen = {n: {} for n in self.E}
        self.dsem = [es.enter_context(nc.semaphore(f"dq{i}")) for i in range(NDS)]
        self.dcnt = [0] * NDS
        self.dnext = 0
        self.dnext_sw = 0
        self.bar = es.enter_context(nc.semaphore("bar"))
        self.barcnt = 0
        self.ninstr = 0

    def wait(self, eng, tok):
        if tok is None:
            return
        key, sem, val, ep = tok
        if ep < self.epoch:
            return
        if key == "pe" and eng == "pe":
            return
        if self.seen[eng].get(key, 0) >= val:
            return
        self.E[eng].wait_ge(sem, val)
        self.seen[eng][key] = val

    def _deps(self, eng, reads, writes, accum=False):
        for b in reads:
            for t in b.w:
                self.wait(eng, t)
            if b.excl:
                for k, t in b.r.items():
                    if k != eng:
                        self.wait(eng, t)
        for b in writes:
            if not accum:
                for t in b.w:
                    self.wait(eng, t)
            for t in b.r.values():
                self.wait(eng, t)

    def _mark(self, tok, reads, writes, accum=False):
        for b in reads:
            b.r[tok[0]] = tok
        for b in writes:
            if accum:
                b.w = [t for t in b.w if t[3] >= self.epoch] + [tok]
            else:
                b.w = [tok]
            b.r = {}

    def op(self, eng, fn, reads=(), writes=()):
        self._deps(eng, reads, writes)
        ins = fn(self.E[eng])
        self.cnt[eng] += 1
        ins.then_inc(self.sem[eng], 1)
        tok = (eng, self.sem[eng], self.cnt[eng], self.epoch)
        self._mark(tok, reads, writes)
        self.ninstr += 1
        return tok

    def dma(self, q, out, in_, reads=(), writes=(), accum=False):
        if q == "pool":
            i = NHW + self.dnext_sw
            self.dnext_sw = (self.dnext_sw + 1) % (NDS - NHW)
        else:
            i = self.dnext
            self.dnext = (self.dnext + 1) % NHW
        if self.dcnt[i] > 0:
            self.wait(q, (("d", i), self.dsem[i], self.dcnt[i], self.epoch))
        self._deps(q, reads, writes, accum)
        ins = self.E[q].dma_start(out=out, in_=in_)
        self.dcnt[i] += 16
        ins.then_inc(self.dsem[i], 16)
        tok = (("d", i), self.dsem[i], self.dcnt[i], self.epoch)
        self._mark(tok, reads, writes, accum)
        self.ninstr += 1
        return tok

    def barrier(self):
        for w in self.E:
            for e in self.E:
                if e != w and self.cnt[e] > 0:
                    self.wait(w, (e, self.sem[e], self.cnt[e], self.epoch))
        for i in range(NDS):
            if self.dcnt[i] > 0:
                self.wait("sp", (("d", i), self.dsem[i], self.dcnt[i], self.epoch))
        nxt = self.semsets[(self.epoch + 1) % 2]
        self.nc.sync.sem_inc(self.bar, 1)
        self.barcnt += 1
        for e in self.E:
            if e != "sp":
                self.E[e].wait_ge(self.bar, self.barcnt)
        if self.epoch >= 1:
            self.nc.all_engine_barrier()
            for e in self.E:
                self.nc.sync.sem_clear(nxt[e])
            self.nc.sync.sem_inc(self.bar, 1)
            self.barcnt += 1
            for e in self.E:
                if e != "sp":
                    self.E[e].wait_ge(self.bar, self.barcnt)
        self.epoch += 1
        self.sem = dict(nxt)
        for e in self.E:
            self.cnt[e] = 0
            self.seen[e] = {}


def _ret_consts():
    h = np.arange(6, dtype=np.float32)
    log_g = np.log1p(-np.exp2(-5.0 - h)).astype(np.float32)
    idx = np.arange(128, dtype=np.float32)
    decayT = np.exp(np.abs(idx[:, None] - idx[None, :])[:, None, :] * log_g[None, :, None]) * 0.125
    wf = np.exp((127.0 - idx)[:, None] * log_g[None, :]) * 0.125
    wb = np.exp(idx[:, None] * log_g[None, :]) * 0.125
    w1 = np.exp((idx + 1.0)[:, None] * log_g[None, :])
    w2 = np.exp((128.0 - idx)[:, None] * log_g[None, :])
    rep = lambda w: np.repeat(w[:, :, None], 64, axis=2)
    wtab = np.stack([rep(wf), rep(wb), rep(w1), rep(w2)], axis=1)
    gch = np.exp(128.0 * log_g)
    gcol = np.zeros((128, 3), np.float32)
    for hp in range(3):
        gcol[:64, hp] = gch[2 * hp]
        gcol[64:, hp] = gch[2 * hp + 1]
    half = 32
    inv_freq = (10000.0 ** (-np.arange(half, dtype=np.float32) / half)).astype(np.float32)
    pos = np.arange(S, dtype=np.float32)
    ang = (pos[:, None] * inv_freq[None, :]).astype(np.float32)
    cos = np.cos(ang).astype(np.float32).reshape(32, 128, 32).transpose(1, 0, 2)
    sin = np.sin(ang).astype(np.float32).reshape(32, 128, 32).transpose(1, 0, 2)
    cc = np.concatenate([cos, cos], axis=2)
    ss = np.concatenate([-sin, sin], axis=2)
    rope = np.stack([cc, ss], axis=1)
    return (np.ascontiguousarray(decayT, np.float32), np.ascontiguousarray(wtab, np.float32),
            gcol, np.ascontiguousarray(rope, np.float32))


def _na_plan():
    R = 64
    rows = np.arange(R)
    rstart = np.clip(rows - 4, 0, R - 8)
    qc = np.arange(64)
    cstart = np.clip(qc - 8, 0, 48)
    kc = np.arange(64)
    colvalid = (kc[:, None] >= cstart[None, :]) & (kc[:, None] < cstart[None, :] + 16)
    dcidx = np.clip(kc[:, None] - qc[None, :], -15, 15) + 15
    blocks = {}
    blk_list = []
    pair_blk = {}
    win = []
    for t in range(32):
        ss = []
        for s in range(32):
            ridx = np.zeros((128, 128), np.int64)
            cidx = np.zeros((128, 128), np.int64)
            val = np.zeros((128, 128), bool)
            for jr in range(2):
                J = 2 * t + jr
                for ir in range(2):
                    Rq = 2 * s + ir
                    rv = (rstart[Rq] <= J) and (J < rstart[Rq] + 8)
                    sl = (slice(jr * 64, jr * 64 + 64), slice(ir * 64, ir * 64 + 64))
                    if rv:
                        val[sl] = colvalid
                        ridx[sl] = J - Rq + 7
                        cidx[sl] = dcidx
            if not val.any():
                continue
            ss.append(s)
            key = (val.tobytes(), (ridx * val).tobytes(), (cidx * val).tobytes())
            if key not in blocks:
                blocks[key] = len(blk_list)
                blk_list.append((ridx * val, cidx * val, val))
            pair_blk[(t, s)] = blocks[key]
        assert ss == list(range(ss[0], ss[-1] + 1))
        win.append((ss[0], ss[-1]))
    order = []
    tmid = 16
    for s in range(win[tmid][0], win[tmid][1] + 1):
        order.append(pair_blk[(tmid, s)])
    for b in range(len(blk_list)):
        if b not in order:
            order.append(b)
    remap = {old: new for new, old in enumerate(order)}
    blk_list = [blk_list[o] for o in order]
    pair_blk = {k: remap[v] for k, v in pair_blk.items()}
    keyt = []
    for s in range(32):
        ts = [t for t in range(32) if win[t][0] <= s <= win[t][1]]
        assert ts == list(range(ts[0], ts[-1] + 1))
        keyt.append((ts[0], ts[-1]))
    return win, keyt, pair_blk, blk_list


_NA_WIN, _NA_KEYT, _NA_PAIR, _NA_BLKS = _na_plan()
NBLK = len(_NA_BLKS)


def _na_bias_host(rpb):
    L = rpb.shape[0]
    out = np.empty((L, 4, 128, NBLK, 128), np.float32)
    for b, (ri, ci, val) in enumerate(_NA_BLKS):
        g = rpb[:, :, ri, ci]
        out[:, :, :, b, :] = np.where(val[None, None], g, np.float32(NEG))
    return out


def build(depth=DEPTH, debug=False):
    nc = bass.Bass("TRN2", target_bir_lowering=False)
    L = depth

    def din(name, shape, dt=F32):
        return nc.dram_tensor(name, list(shape), dt, kind="ExternalInput").ap()

    x_in = din("x", [S, D])
    w_in = din("w_in", [L, D, INW])
    w_out = din("w_out", [L, D, D])
    w_gate = din("w_gate", [L, D, DFF])
    w_up = din("w_up", [L, D, DFF])
    w_down = din("w_down", [L, DFF, D])
    lru_cw = din("lru_cw", [L, 3, 128, 4])
    lru_vec = din("lru_vec", [L, 3, 128, 7])
    lru_wg = din("lru_wg", [L, 3, 128, 4, 128])
    ret_gnw = din("ret_gnw", [L, 128, 384])
    na_bias = din("na_bias", [L, 4, 128, NBLK, 128])
    ln_gb = din("ln_gb", [L, 128, 4, D])
    c_ident = din("c_ident", [128, 128])
    c_decay = din("c_decay", [128, 6, 128])
    c_wtab = din("c_wtab", [128, 4, 6, 64])
    c_gcol = din("c_gcol", [128, 3])
    c_rope = din("c_rope", [128, 2, 32, 64])
    out = nc.dram_tensor("out", [S, D], F32, kind="ExternalOutput").ap()

    dk = "ExternalOutput" if debug else "Internal"

    def dscr(name, shape, dt, kind="Internal"):
        return nc.dram_tensor(name, list(shape), dt, kind=kind).ap()

    s_wfm = dscr("s_wfm", [L, 10, 128, 8, 128], BF16)
    s_wret = dscr("s_wret", [L, 3, 128, 8, 512], BF16)
    s_wnav = dscr("s_wnav", [L, 128, 8, 256], BF16)
    s_wo = dscr("s_wo", [L, 128, 8, D], BF16)
    s_wg = dscr("s_wg", [L, NF, 128, 8, 128], BF16)
    s_wu = dscr("s_wu", [L, NF, 128, 8, 128], BF16)
    s_wd = dscr("s_wd", [L, 128, NF, D], BF16)
    xT_d = dscr("xT_d", [D, S], BF16)
    yT_d = dscr("yT_d", [D, S], BF16, kind=dk)
    xres_d = dscr("xres_d", [S, D], F32)

    with contextlib.ExitStack() as es:
        kb = KB(nc, es)
        sbt = lambda name, shape, dt: es.enter_context(nc.sbuf_tensor(name, list(shape), dt))
        identf = sbt("identf", [128, 128], F32)
        identb = sbt("identb", [128, 128], BF16)
        c_m05 = sbt("c_m05", [128, 8], F32)
        c_p05 = sbt("c_p05", [128, 512], F32)
        b_const = Buf()
        kb.dma("sp", identf[:], c_ident, writes=[b_const])
        kb.dma("pool", identb[:], c_ident, writes=[b_const])
        kb.op("pool", lambda e: e.memset(c_m05[:], -0.5), writes=[b_const])
        kb.op("pool", lambda e: e.memset(c_p05[:], 0.5), writes=[b_const])

        wb = {k: [Buf() for _ in range(L)] for k in ("fm", "ret", "nav", "wo", "wg", "wu", "wd")}

        def cast_list(l):
            jobs = []
            wv = w_in[l].rearrange("(kc p) n -> p kc n", p=128)
            cols = [0, 128, 256, 384, 512, 640, 2304, 2432, 2560, 2688]
            for t, c0 in enumerate(cols):
                jobs.append(lambda o_i=(s_wfm[l, t], wv[:, :, c0:c0 + 128]), bb=wb["fm"][l]: kb.dma("pool", o_i[0], o_i[1], writes=[bb], accum=True))
            for hp in range(3):
                for gi, g0 in enumerate([768, 1152, 1536, 1920]):
                    c0 = g0 + hp * 128
                    jobs.append(lambda o_i=(s_wret[l, hp, :, :, gi * 128:(gi + 1) * 128], wv[:, :, c0:c0 + 128]), bb=wb["ret"][l]: kb.dma("pool", o_i[0], o_i[1], writes=[bb], accum=True))
            jobs.append(lambda o_i=(s_wnav[l], wv[:, :, 2816:3072]), bb=wb["nav"][l]: kb.dma("pool", o_i[0], o_i[1], writes=[bb], accum=True))
            wov = w_out[l].rearrange("(kc p) n -> p kc n", p=128)
            for kc in range(0, 8, 2):
                jobs.append(lambda o_i=(s_wo[l, :, kc:kc + 2, :], wov[:, kc:kc + 2, :]), bb=wb["wo"][l]: kb.dma("pool", o_i[0], o_i[1], writes=[bb], accum=True))
            wgv = w_gate[l].rearrange("(kc p) n -> p kc n", p=128)
            wuv = w_up[l].rearrange("(kc p) n -> p kc n", p=128)
            for f in range(NF):
                jobs.append(lambda o_i=(s_wg[l, f], wgv[:, :, f * 128:(f + 1) * 128]), bb=wb["wg"][l]: kb.dma("pool", o_i[0], o_i[1], writes=[bb], accum=True))
                jobs.append(lambda o_i=(s_wu[l, f], wuv[:, :, f * 128:(f + 1) * 128]), bb=wb["wu"][l]: kb.dma("pool", o_i[0], o_i[1], writes=[bb], accum=True))
            wdv = w_down[l].rearrange("(fc p) n -> p fc n", p=128)
            for f in range(0, NF, 2):
                jobs.append(lambda o_i=(s_wd[l, :, f:f + 2, :], wdv[:, f:f + 2, :]), bb=wb["wd"][l]: kb.dma("pool", o_i[0], o_i[1], writes=[bb], accum=True))

            return jobs

        for j in cast_list(0):
            j()

        b_xT_d = bufs(NB)
        b_yT_d = [Buf() for _ in range(8)]
        b_xres = bufs(NT)

        with contextlib.ExitStack() as ps:
            sb = lambda name, shape, dt: ps.enter_context(nc.sbuf_tensor(name, list(shape), dt))
            xt = [sb(f"p_xt{i}", [128, D], F32) for i in range(2)]
            xo = [sb(f"p_xo{i}", [128, 8, 512], BF16) for i in range(2)]
            pt = [ps.enter_context(nc.psum_tensor(f"p_pt{i}", [128, 4, 128], F32)) for i in range(2)]
            b_xt, b_pt = bufs(2), pbufs(2)
            b_xo = [bufs(2), bufs(2)]
            for tt in range(NT):
                tb, j = divmod(tt, 4)
                xi = tt % 2
                kb.dma("sp", xt[xi][:], x_in[tt * 128:(tt + 1) * 128, :], writes=[b_xt[xi]])
                for hf in range(2):
                    for q in range(4):
                        kc = hf * 4 + q
                        kb.op("pe", lambda e, kc=kc, q=q, hf=hf: e.transpose(pt[hf][:, q, :], xt[xi][:, kc * 128:(kc + 1) * 128], identf[:]),
                              reads=[b_xt[xi], b_const], writes=[b_pt[hf]])
                    eng = "act" if hf == 0 else "dve"
                    if eng == "act":
                        kb.op("act", lambda e, hf=hf: e.copy(out=xo[tb % 2][:, hf * 4:hf * 4 + 4, j * 128:(j + 1) * 128], in_=pt[hf][:]),
                              reads=[b_pt[hf]], writes=[b_xo[tb % 2][hf]])
                    else:
                        kb.op("dve", lambda e, hf=hf: e.tensor_copy(out=xo[tb % 2][:, hf * 4:hf * 4 + 4, j * 128:(j + 1) * 128], in_=pt[hf][:]),
                              reads=[b_pt[hf]], writes=[b_xo[tb % 2][hf]])
                if j == 3:
                    kb.dma("sp", xT_d.rearrange("(kc p) t -> p kc t", p=128)[:, :, tb * 512:(tb + 1) * 512], xo[tb % 2][:],
                           reads=b_xo[tb % 2], writes=[b_xT_d[tb]])
        kb.barrier()

        for l in range(L):
            next_casts = cast_list(l + 1) if l + 1 < L else []
            xsrc = x_in if l == 0 else xres_d
            xdst = out if l == L - 1 else xres_d
            with contextlib.ExitStack() as ms:
                xT = ms.enter_context(nc.sbuf_tensor(f"xT_{l}", [128, 8, S], BF16))
                b_xT = bufs(NB)
                xTv = xT_d.rearrange("(kc p) t -> p kc t", p=128)
                for tb in range(NB):
                    kb.dma("sp" if tb % 2 == 0 else "act", xT[:, :, tb * 512:(tb + 1) * 512], xTv[:, :, tb * 512:(tb + 1) * 512],
                           reads=[b_xT_d[tb]], writes=[b_xT[tb]])
                phase_lru(nc, kb, l, xT, b_xT, s_wfm, wb, lru_cw, lru_vec, lru_wg, yT_d, b_yT_d, c_p05, b_const)
                kb.barrier()
                phase_ret(nc, kb, l, xT, b_xT, s_wret, wb, ret_gnw, c_decay, c_wtab, c_gcol, c_rope, yT_d, b_yT_d,
                          identb, c_m05, b_const)
                kb.barrier()
                phase_na(nc, kb, l, xT, b_xT, s_wfm, s_wnav, wb, na_bias, yT_d, b_yT_d, identb, b_const)
                kb.barrier()
            phase_ffn(nc, kb, l, L, s_wo, s_wg, s_wu, s_wd, wb, ln_gb, xsrc, xdst, b_xres, yT_d, b_yT_d, xT_d, b_xT_d,
                      identf, c_m05, b_const, next_casts)
            kb.barrier()
        print("instructions:", kb.ninstr, "semaphores:", kb.nsem + NDS + 1)
    return nc


def phase_lru(nc, kb, l, xT, b_xT, s_wfm, wb, lru_cw, lru_vec, lru_wg, yT_d, b_yT_d, c_p05, b_const):
    with contextlib.ExitStack() as ps:
        sb = lambda name, shape, dt: ps.enter_context(nc.sbuf_tensor(f"{name}_L{l}", list(shape), dt))
        pp = lambda name, shape, dt: ps.enter_context(nc.psum_tensor(f"{name}_L{l}", list(shape), dt))
        LX = sb("l_lx", [128, S + 4], F32)
        XC = sb("l_xc", [128, S], F32)
        XCB = sb("l_xcb", [128, S], BF16)
        GG = sb("l_gg", [128, S], BF16)
        YB = sb("l_yb", [128, S], BF16)
        wx = sb("l_wx", [128, 8, 128], BF16)
        wg_ = sb("l_wg", [128, 8, 128], BF16)
        gm = sb("l_gm", [128, 4, 128], BF16)
        cw = sb("l_cw", [128, 4], F32)
        vec = sb("l_vec", [128, 7], F32)
        der = sb("l_der", [128, 12], F32)
        NSET = 4
        TR = [sb(f"l_tr{i}", [128, 512], F32) for i in range(NSET)]
        TI = [sb(f"l_ti{i}", [128, 512], F32) for i in range(NSET)]
        TA = [sb(f"l_ta{i}", [128, 512], F32) for i in range(NSET)]
        TH = [sb(f"l_th{i}", [128, 512], F32) for i in range(NSET)]
        TQ = [sb(f"l_tq{i}", [128, 512], F32) for i in range(NSET)]
        TU = [sb(f"l_tu{i}", [128, 512], F32) for i in range(NSET)]
        HT = [sb(f"l_ht{i}", [128, 512], F32) for i in range(NSET)]
        P = [pp(f"l_p{i}", [128, 512], F32) for i in range(4)]
        b_P = pbufs(4)
        b_TR, b_TI, b_TA, b_TH, b_TQ, b_TU, b_HT = (bufs(NSET) for _ in range(7))
        b_w = Buf()
        b_sm = Buf()
        b_LX = bufs(NB)
        b_halo = Buf()
        b_XC, b_XCB, b_GG, b_YB = bufs(NB), bufs(NB), bufs(NB), bufs(NB)
        pcnt = 0
        for ct in range(3):
            kb.dma("sp", wx[:], s_wfm[l, ct], reads=[wb["fm"][l]], writes=[b_w])
            kb.dma("act", wg_[:], s_wfm[l, 3 + ct], reads=[wb["fm"][l]], writes=[b_w])
            kb.dma("pool", gm[:], lru_wg[l, ct], writes=[b_w])
            kb.dma("sp", cw[:], lru_cw[l, ct], writes=[b_sm])
            kb.dma("sp", vec[:], lru_vec[l, ct], writes=[b_sm])
            kb.op("act", lambda e: e.activation(out=der[:, 0:2], in_=vec[:, 5:7], func=AF.Exp, scale=-1.0), reads=[b_sm], writes=[b_sm])
            kb.op("act", lambda e: e.activation(out=der[:, 2:4], in_=der[:, 0:2], func=AF.Ln, bias=1.0, scale=1.0), reads=[b_sm], writes=[b_sm])
            kb.op("dve", lambda e: e.tensor_scalar(out=der[:, 4:6], in0=der[:, 2:4], scalar1=-4.0, scalar2=None, op0=ALU.mult), reads=[b_sm], writes=[b_sm])
            kb.op("dve", lambda e: e.tensor_scalar(out=der[:, 6:8], in0=der[:, 2:4], scalar1=4.0, scalar2=None, op0=ALU.mult), reads=[b_sm], writes=[b_sm])
            kb.op("dve", lambda e: e.tensor_scalar(out=der[:, 8:12], in0=vec[:, 1:5], scalar1=0.5, scalar2=None, op0=ALU.mult), reads=[b_sm], writes=[b_sm])
            kb.op("pool", lambda e: e.memset(LX[:, 0:2], 0.0), writes=[b_halo])
            kb.op("pool", lambda e: e.memset(LX[:, S + 2:S + 4], 0.0), writes=[b_halo])
            for blk in range(NB):
                sl = slice(blk * 512, (blk + 1) * 512)
                pi = pcnt % 4; pcnt += 1
                for kc in range(8):
                    kb.op("pe", lambda e, kc=kc, pi=pi: e.matmul(P[pi][:], lhsT=wx[:, kc, :], rhs=xT[:, kc, sl], start=(kc == 0), stop=(kc == 7)),
                          reads=[b_w, b_xT[blk]], writes=[b_P[pi]])
                kb.op("act", lambda e, pi=pi: e.copy(out=LX[:, 2 + blk * 512:2 + (blk + 1) * 512], in_=P[pi][:]), reads=[b_P[pi]], writes=[b_LX[blk]])
                pi = pcnt % 4; pcnt += 1
                for kc in range(8):
                    kb.op("pe", lambda e, kc=kc, pi=pi: e.matmul(P[pi][:], lhsT=wg_[:, kc, :], rhs=xT[:, kc, sl], start=(kc == 0), stop=(kc == 7)),
                          reads=[b_w, b_xT[blk]], writes=[b_P[pi]])
                kb.op("act", lambda e, pi=pi: e.activation(out=GG[:, sl], in_=P[pi][:], func=AF.Gelu), reads=[b_P[pi]], writes=[b_GG[blk]])
            for blk in range(NB):
                sl = slice(blk * 512, (blk + 1) * 512)
                rd = [b_LX[blk], b_LX[min(blk + 1, NB - 1)], b_LX[max(blk - 1, 0)], b_halo, b_sm]
                kb.op("dve", lambda e: e.tensor_scalar(out=XC[:, sl], in0=LX[:, blk * 512:blk * 512 + 512], scalar1=cw[:, 0:1], scalar2=vec[:, 0:1], op0=ALU.mult, op1=ALU.add),
                      reads=rd, writes=[b_XC[blk]])
                for j in range(1, 4):
                    kb.op("dve", lambda e, j=j: e.scalar_tensor_tensor(out=XC[:, sl], in0=LX[:, blk * 512 + j:blk * 512 + j + 512], scalar=cw[:, j:j + 1], in1=XC[:, sl], op0=ALU.mult, op1=ALU.add),
                          reads=rd, writes=[b_XC[blk]])
                kb.op("pool", lambda e: e.tensor_copy(out=XCB[:, sl], in_=XC[:, sl]), reads=[b_XC[blk]], writes=[b_XCB[blk]])
            items = [(0, blk) for blk in range(NB)] + [(1, blk) for blk in range(NB - 1, -1, -1)]

            def s0(i):
                d, blk = items[i]
                sl = slice(blk * 512, (blk + 1) * 512)
                si = i % NSET
                pr = (2 * i) % 4
                pi_ = (2 * i + 1) % 4
                kb.op("pe", lambda e: e.matmul(P[pr][:], lhsT=gm[:, d, :], rhs=XCB[:, sl], start=True, stop=True), reads=[b_w, b_XCB[blk]], writes=[b_P[pr]])
                kb.op("pe", lambda e: e.matmul(P[pi_][:], lhsT=gm[:, 2 + d, :], rhs=XCB[:, sl], start=True, stop=True), reads=[b_w, b_XCB[blk]], writes=[b_P[pi_]])
                kb.op("act", lambda e: e.activation(out=TR[si][:], in_=P[pr][:], func=AF.Tanh, bias=der[:, 8 + d:9 + d], scale=0.5), reads=[b_P[pr], b_sm], writes=[b_TR[si]])
                kb.op("act", lambda e: e.activation(out=TI[si][:], in_=P[pi_][:], func=AF.Tanh, bias=der[:, 10 + d:11 + d], scale=0.5), reads=[b_P[pi_], b_sm], writes=[b_TI[si]])
                kb.op("act", lambda e: e.activation(out=TA[si][:], in_=TR[si][:], func=AF.Exp, bias=der[:, 4 + d:5 + d], scale=der[:, 4 + d:5 + d]), reads=[b_TR[si], b_sm], writes=[b_TA[si]])
                kb.op("act", lambda e: e.activation(out=TH[si][:], in_=TR[si][:], func=AF.Tanh, bias=der[:, 6 + d:7 + d], scale=der[:, 6 + d:7 + d]), reads=[b_TR[si], b_sm], writes=[b_TH[si]])
                kb.op("dve", lambda e: e.tensor_tensor(out=TQ[si][:], in0=TA[si][:], in1=TA[si][:], op=ALU.mult), reads=[b_TA[si]], writes=[b_TQ[si]])
                kb.op("dve", lambda e: e.scalar_tensor_tensor(out=TQ[si][:], in0=TQ[si][:], scalar=1.0, in1=TH[si][:], op0=ALU.add, op1=ALU.mult), reads=[b_TH[si]], writes=[b_TQ[si]])
                kb.op("dve", lambda e: e.scalar_tensor_tensor(out=TU[si][:], in0=TI[si][:], scalar=1.0, in1=XC[:, sl], op0=ALU.add, op1=ALU.mult), reads=[b_TI[si], b_XC[blk]], writes=[b_TU[si]])

            def s1(i):
                d, blk = items[i]
                sl = slice(blk * 512, (blk + 1) * 512)
                si = i % NSET
                first = (i == 0 or i == NB)
                kb.op("dve", lambda e: e.scalar_tensor_tensor(out=TU[si][:], in0=TU[si][:], scalar=0.5, in1=TQ[si][:], op0=ALU.mult, op1=ALU.mult), reads=[b_TQ[si]], writes=[b_TU[si]])
                if d == 0:
                    o_ap = LX[:, 2 + blk * 512:2 + (blk + 1) * 512]
                    init = 0.0 if first else LX[:, 2 + blk * 512 - 1:2 + blk * 512]
                    rd = [b_TA[si], b_TU[si]] + ([] if first else [b_LX[blk - 1]])
                    kb.op("dve", lambda e: e.tensor_tensor_scan(out=o_ap, data0=TA[si][:], data1=TU[si][:], initial=init, op0=ALU.mult, op1=ALU.add),
                          reads=rd, writes=[b_LX[blk]])
                else:
                    sp_ = (i - 1) % NSET
                    init = 0.0 if first else HT[sp_][:, 0:1]
                    rd = [b_TA[si], b_TU[si]] + ([] if first else [b_HT[sp_]])
                    kb.op("dve", lambda e: e.tensor_tensor_scan(out=HT[si][:, ::-1], data0=TA[si][:, ::-1], data1=TU[si][:, ::-1], initial=init, op0=ALU.mult, op1=ALU.add),
                          reads=rd, writes=[b_HT[si]])
                    kb.op("pool", lambda e: e.tensor_tensor(out=TH[si][:], in0=HT[si][:], in1=LX[:, 2 + blk * 512:2 + (blk + 1) * 512], op=ALU.add),
                          reads=[b_HT[si], b_LX[blk]], writes=[b_TH[si]])
                    kb.op("pool", lambda e: e.tensor_tensor(out=YB[:, sl], in0=TH[si][:], in1=GG[:, sl], op=ALU.mult),
                          reads=[b_TH[si], b_GG[blk]], writes=[b_YB[blk]])

            def s1_sqrt(i):
                si = i % NSET
                kb.op("act", lambda e: e.activation(out=TQ[si][:], in_=TQ[si][:], func=AF.Sqrt), writes=[b_TQ[si]])

            npairs = len(items) // 2
            for p in range(npairs + 1):
                if p < npairs:
                    s0(2 * p)
                    s0(2 * p + 1)
                if p >= 1:
                    s1_sqrt(2 * p - 2)
                    s1_sqrt(2 * p - 1)
                    s1(2 * p - 2)
                    s1(2 * p - 1)
            kb.dma("sp", yT_d[ct * 128:(ct + 1) * 128, :], YB[:], reads=b_YB, writes=[b_yT_d[ct]])


def phase_ret(nc, kb, l, xT, b_xT, s_wret, wb, ret_gnw, c_decay, c_wtab, c_gcol, c_rope, yT_d, b_yT_d, identb, c_m05, b_const):
    with contextlib.ExitStack() as ps:
        sb = lambda name, shape, dt: ps.enter_context(nc.sbuf_tensor(f"{name}_L{l}", list(shape), dt))
        pp = lambda name, shape, dt: ps.enter_context(nc.psum_tensor(f"{name}_L{l}", list(shape), dt))
        decay = sb("r_decay", [128, 6, 128], F32)
        wtab = sb("r_wtab", [128, 4, 6, 64], F32)
        gcol = sb("r_gcol", [128, 3], F32)
        rope = sb("r_rope", [128, 2, 32, 64], F32)
        gnw = sb("r_gnw", [128, 384], F32)
        W = sb("r_w", [128, 8, 512], BF16)
        qT = sb("r_qT", [128, S], BF16)
        kT = sb("r_kT", [128, S], BF16)
        VB = sb("r_vb", [128, NT, 128], BF16)
        SG = sb("r_sg", [128, NT, 128], F32)
        STF = sb("r_stf", [128, NT, 64], BF16)
        STB = sb("r_stb", [128, NT, 64], BF16)
        KVB = sb("r_kvb", [128, NT, 64], F32)
        ST32 = sb("r_st32", [128, 64], F32)
        YB = sb("r_yb", [128, S], BF16)
        NSET = 4
        M1 = [sb(f"r_m1{i}", [128, 2, 128], F32) for i in range(NSET)]
        M2 = [sb(f"r_m2{i}", [128, 2, 128], F32) for i in range(NSET)]
        QK = [sb(f"r_qk{i}", [128, 2, 128], BF16) for i in range(NSET)]
        VFB = [sb(f"r_vfb{i}", [128, 2, 2, 64], BF16) for i in range(NSET)]
        STM = [sb(f"r_stm{i}", [128, 2, 128], BF16) for i in range(NSET)]
        OT = [sb(f"r_ot{i}", [128, 2, 64], F32) for i in range(NSET)]
        ASB = [sb(f"r_asb{i}", [128, 2, 64], F32) for i in range(NSET)]
        O2 = [sb(f"r_o2{i}", [128, 2, 64], F32) for i in range(NSET)]
        ST6 = [sb(f"r_st6{i}", [128, 2, 6], F32) for i in range(NSET)]
        MV = [sb(f"r_mv{i}", [128, 2, 2], F32) for i in range(NSET)]
        RS = [sb(f"r_rs{i}", [128, 2], F32) for i in range(NSET)]
        YT = [sb(f"r_yt{i}", [128, 128], BF16) for i in range(NSET)]
        BK = [pp(f"r_bk{i}", [128, 512], F32) for i in range(6)]
        b_BK = pbufs(6)
        PT_t = pp("r_pt", [128, 2, 4, 128], BF16)
        PT = [PT_t[:, 0], PT_t[:, 1]]
        b_ptt = Buf(True)
        b_M1, b_M2, b_QK, b_RS, b_YT = (bufs(NSET) for _ in range(5))
        b_VFB, b_STM, b_OT, b_O2, b_ST6, b_MV, b_ASB = ([bufs(2) for _ in range(NSET)] for _ in range(7))
        b_c = Buf(); b_W = Buf()
        b_qT, b_kT, b_VB, b_SG, b_STF, b_STB, b_KVB, b_YB = (bufs(NT) for _ in range(8))
        b_st = Buf()
        kb.dma("sp", decay[:], c_decay, writes=[b_c], accum=True)
        kb.dma("act", wtab[:], c_wtab, writes=[b_c], accum=True)
        kb.dma("sp", gcol[:], c_gcol, writes=[b_c], accum=True)
        kb.dma("act", rope[:], c_rope, writes=[b_c], accum=True)
        kb.dma("sp", gnw[:], ret_gnw[l], writes=[b_c], accum=True)

        def pipeline(stages, n):
            last = max(sk for sk, _ in stages)
            for i in range(n + last):
                for sk, fn in stages:
                    c = i - sk
                    if 0 <= c < n:
                        fn(c)

        for hp in range(3):
            kb.dma("sp", W[:], s_wret[l, hp], reads=[wb["ret"][l]], writes=[b_W])
            kb.op("pool", lambda e: e.memset(ST32[:], 0.0), writes=[b_st])

            def p1_proj(c):
                cs = slice(c * 128, (c + 1) * 128)
                pa = BK[c % 2]
                for kc in range(8):
                    kb.op("pe", lambda e, kc=kc: e.matmul(pa[:], lhsT=xT[:, kc, cs], rhs=W[:, kc, :], start=(kc == 0), stop=(kc == 7)),
                          reads=[b_xT[c // 4], b_W], writes=[b_BK[c % 2]])

            def p1_elem(c):
                si = c % NSET
                pa = BK[c % 2]; bpa = b_BK[c % 2]
                pv = pa[:, 0:256].rearrange("p (g t f) -> p g t f", g=4, t=2)
                pvs = pv[:, :, ::-1, :]
                ccv = rope[:, 0, c, :].rearrange("p (t f) -> p t f", t=2).unsqueeze(1).broadcast_to([128, 4, 2, 32])
                ssv = rope[:, 1, c, :].rearrange("p (t f) -> p t f", t=2).unsqueeze(1).broadcast_to([128, 4, 2, 32])
                m1v = M1[si][:].rearrange("p a (g t f) -> p (a g) t f", g=2, t=2)
                m2v = M2[si][:].rearrange("p a (g t f) -> p (a g) t f", g=2, t=2)
                kb.op("dve", lambda e: e.tensor_tensor(out=m1v, in0=pv, in1=ccv, op=ALU.mult), reads=[bpa, b_c], writes=[b_M1[si]])
                kb.op("dve", lambda e: e.tensor_tensor(out=m2v, in0=pvs, in1=ssv, op=ALU.mult), reads=[bpa, b_c], writes=[b_M2[si]])
                vv = pa[:, 256:384].rearrange("p (h e) -> p h e", h=2)
                kb.op("dve", lambda e: e.tensor_tensor(out=VFB[si][:, :, 0, :], in0=vv, in1=wtab[:, 0, 2 * hp:2 * hp + 2, :], op=ALU.mult), reads=[bpa, b_c], writes=[b_VFB[si][0]])
                kb.op("dve", lambda e: e.tensor_tensor(out=VFB[si][:, :, 1, :], in0=vv, in1=wtab[:, 1, 2 * hp:2 * hp + 2, :], op=ALU.mult), reads=[bpa, b_c], writes=[b_VFB[si][1]])
                kb.op("pool", lambda e: e.tensor_tensor(out=QK[si][:], in0=M1[si][:], in1=M2[si][:], op=ALU.add), reads=[b_M1[si], b_M2[si]], writes=[b_QK[si]])
                kb.op("act", lambda e: e.copy(out=VB[:, c, :], in_=pa[:, 256:384]), reads=[bpa], writes=[b_VB[c]])
                kb.op("act", lambda e: e.activation(out=SG[:, c, :], in_=pa[:, 384:512], func=AF.Silu), reads=[bpa], writes=[b_SG[c]])
                kb.op("pool", lambda e: e.tensor_tensor(out=SG[:, c, :], in0=SG[:, c, :], in1=gnw[:, hp * 128:(hp + 1) * 128], op=ALU.mult), reads=[b_c], writes=[b_SG[c]])

            def p1_pe2(c):
                si = c % NSET
                cs = slice(c * 128, (c + 1) * 128)
                ptt = PT[c % 2]
                for a in range(2):
                    kb.op("pe", lambda e, a=a: e.transpose(ptt[:, a, :], QK[si][:, a, :], identb[:]), reads=[b_QK[si], b_const], writes=[b_ptt])
                kb.op("act", lambda e: e.copy(out=qT[:, cs], in_=ptt[:, 0, :]), reads=[b_ptt], writes=[b_qT[c]])
                kb.op("act", lambda e: e.copy(out=kT[:, cs], in_=ptt[:, 1, :]), reads=[b_ptt], writes=[b_kT[c]])
                pk = BK[2 + c % 2][:, 0:128].rearrange("p (a e) -> p a e", a=2)
                for h in range(2):
                    kb.op("pe", lambda e, h=h: e.matmul(pk[h * 64:(h + 1) * 64, :, :], lhsT=QK[si][:, 1, h * 64:(h + 1) * 64], rhs=VFB[si][:, h, :, :], start=True, stop=True),
                          reads=[b_QK[si]] + b_VFB[si], writes=[b_BK[2 + c % 2]])

            def p1_state(c):
                pk = BK[2 + c % 2][:, 0:128].rearrange("p (a e) -> p a e", a=2)
                kb.op("pool", lambda e: e.tensor_copy(out=STF[:, c, :], in_=ST32[:]), reads=[b_st], writes=[b_STF[c]])
                kb.op("dve", lambda e: e.scalar_tensor_tensor(out=ST32[:], in0=ST32[:], scalar=gcol[:, hp:hp + 1], in1=pk[:, 0, :], op0=ALU.mult, op1=ALU.add),
                      reads=[b_BK[2 + c % 2], b_c, b_STF[c]], writes=[b_st])
                kb.op("dve", lambda e: e.tensor_copy(out=KVB[:, c, :], in_=pk[:, 1, :]), reads=[b_BK[2 + c % 2]], writes=[b_KVB[c]])

            pipeline([(0, p1_proj), (1, p1_elem), (1, p1_pe2), (2, p1_state)], NT)

            kb.op("pool", lambda e: e.memset(ST32[:], 0.0), reads=[b_STF[NT - 1]], writes=[b_st])
            for c in range(NT - 1, -1, -1):
                kb.op("pool", lambda e, c=c: e.tensor_copy(out=STB[:, c, :], in_=ST32[:]), reads=[b_st], writes=[b_STB[c]])
                kb.op("dve", lambda e, c=c: e.scalar_tensor_tensor(out=ST32[:], in0=ST32[:], scalar=gcol[:, hp:hp + 1], in1=KVB[:, c, :], op0=ALU.mult, op1=ALU.add),
                      reads=[b_KVB[c], b_c, b_STB[c]], writes=[b_st])

            def cc(i):
                return NT - 1 - i

            def p2_scores(i):
                c = cc(i); cs = slice(c * 128, (c + 1) * 128); si = i % NSET
                for h in range(2):
                    hs = slice(h * 64, (h + 1) * 64)
                    kb.op("pe", lambda e, h=h, hs=hs: e.matmul(BK[h][:, 0:128], lhsT=kT[hs, cs], rhs=qT[hs, cs], start=True, stop=True),
                          reads=[b_kT[c], b_qT[c]], writes=[b_BK[h]])
                for h in range(2):
                    kb.op("dve", lambda e, h=h: e.tensor_tensor(out=STM[si][:, h, :], in0=BK[h][:, 0:128], in1=decay[:, 2 * hp + h, :], op=ALU.mult),
                          reads=[b_BK[h], b_c], writes=[b_STM[si][h]])

            def p2_out(i):
                c = cc(i); cs = slice(c * 128, (c + 1) * 128); si = i % NSET
                st = 2 + 2 * (i % 2)
                for h in range(2):
                    hs = slice(h * 64, (h + 1) * 64)
                    po = BK[st + h]
                    kb.op("pe", lambda e, h=h, hs=hs, po=po: e.matmul(po[:, 0:64], lhsT=STM[si][:, h, :], rhs=VB[:, c, hs], start=True, stop=True),
                          reads=[b_STM[si][h], b_VB[c]], writes=[b_BK[st + h]])
                    kb.op("pe", lambda e, h=h, hs=hs, po=po: e.matmul(po[:, 64:128], lhsT=qT[hs, cs], rhs=STF[hs, c, :], start=True, stop=True),
                          reads=[b_qT[c], b_STF[c]], writes=[b_BK[st + h]])
                    kb.op("pe", lambda e, h=h, hs=hs, po=po: e.matmul(po[:, 128:192], lhsT=qT[hs, cs], rhs=STB[hs, c, :], start=True, stop=True),
                          reads=[b_qT[c], b_STB[c]], writes=[b_BK[st + h]])
                for h in range(2):
                    po = BK[st + h]
                    kb.op("act", lambda e, h=h, po=po: e.copy(out=ASB[si][:, h, :], in_=po[:, 0:64]), reads=[b_BK[st + h]], writes=[b_ASB[si][h]])
                for h in range(2):
                    hg = 2 * hp + h
                    po = BK[st + h]
                    kb.op("dve", lambda e, h=h, hg=hg, po=po: e.scalar_tensor_tensor(out=O2[si][:, h, :], in0=po[:, 128:192], scalar=wtab[:, 3, hg, 0:1], in1=ASB[si][:, h, :], op0=ALU.mult, op1=ALU.add),
                          reads=[b_BK[st + h], b_c, b_ASB[si][h]], writes=[b_O2[si][h]])
                for h in range(2):
                    hg = 2 * hp + h
                    po = BK[st + h]
                    kb.op("dve", lambda e, h=h, hg=hg, po=po: e.scalar_tensor_tensor(out=OT[si][:, h, :], in0=po[:, 64:128], scalar=wtab[:, 2, hg, 0:1], in1=O2[si][:, h, :], op0=ALU.mult, op1=ALU.add),
                          reads=[b_BK[st + h], b_c, b_O2[si][h]], writes=[b_OT[si][h]])
                for h in range(2):
                    kb.op("dve", lambda e, h=h: e.bn_stats(out=ST6[si][:, h, :], in_=OT[si][:, h, :]), reads=[b_OT[si][h]], writes=[b_ST6[si][h]])
                for h in range(2):
                    kb.op("dve", lambda e, h=h: e.bn_aggr(out=MV[si][:, h, :], in_=ST6[si][:, h, :]), reads=[b_ST6[si][h]], writes=[b_MV[si][h]])
                kb.op("pool", lambda e: e.tensor_scalar(out=RS[si][:], in0=MV[si][:, :, 1], scalar1=GN_EPS, scalar2=None, op0=ALU.add), reads=b_MV[si], writes=[b_RS[si]])
                kb.op("pool", lambda e: e.tensor_tensor(out=RS[si][:], in0=RS[si][:], in1=c_m05[:, 0:2], op=ALU.pow), reads=[b_const], writes=[b_RS[si]])

            def p2_norm(i):
                c = cc(i); si = i % NSET
                for h in range(2):
                    kb.op("dve", lambda e, h=h: e.tensor_scalar(out=OT[si][:, h, :], in0=OT[si][:, h, :], scalar1=MV[si][:, h, 0:1], scalar2=RS[si][:, h:h + 1], op0=ALU.subtract, op1=ALU.mult),
                          reads=[b_MV[si][h], b_RS[si]], writes=[b_OT[si][h]])
                kb.op("pool", lambda e: e.tensor_tensor(out=YT[si][:], in0=OT[si][:].rearrange("p h e -> p (h e)"), in1=SG[:, c, :], op=ALU.mult), reads=b_OT[si] + [b_SG[c]], writes=[b_YT[si]])

            def p2_tr(i):
                c = cc(i); cs = slice(c * 128, (c + 1) * 128); si = i % NSET
                ptt = PT[i % 2]
                kb.op("pe", lambda e: e.transpose(ptt[:, 0, :], YT[si][:], identb[:]), reads=[b_YT[si], b_const], writes=[b_ptt])
                kb.op("act", lambda e: e.copy(out=YB[:, cs], in_=ptt[:, 0, :]), reads=[b_ptt], writes=[b_YB[c]])

            pipeline([(0, p2_scores), (1, p2_out), (2, p2_norm), (3, p2_tr)], NT)
            kb.dma("sp", yT_d[384 + hp * 128:384 + (hp + 1) * 128, :], YB[:], reads=b_YB, writes=[b_yT_d[3 + hp]])


def phase_na(nc, kb, l, xT, b_xT, s_wfm, s_wnav, wb, na_bias, yT_d, b_yT_d, identb, b_const):
    with contextlib.ExitStack() as ps:
        sb = lambda name, shape, dt: ps.enter_context(nc.sbuf_tensor(f"{name}_L{l}", list(shape), dt))
        pp = lambda name, shape, dt: ps.enter_context(nc.psum_tensor(f"{name}_L{l}", list(shape), dt))
        QKt = [sb(f"n_qk{i}", [128, S], BF16) for i in range(4)]
        VA = sb("n_va", [128, NT, 4, 65], BF16)
        BIAS = sb("n_bias", [128, NBLK, 128], F32)
        Wt = [sb(f"n_w{i}", [128, 8, 128], BF16) for i in range(2)]
        Wv = sb("n_wv", [128, 8, 256], BF16)
        NR = 7
        PTr = [sb(f"n_pt{i}", [128, 768], BF16) for i in range(NR)]
        EP = [sb(f"n_ep{i}", [128, 768], F32) for i in range(2)]
        YTK = sb("n_ytk", [128, NT, 256], BF16)
        RC = [sb(f"n_rc{i}", [128, 1], F32) for i in range(2)]
        PTb_t = pp("n_ptb", [128, 2, 4, 128], BF16)
        PTb = [PTb_t[:, 0], PTb_t[:, 1]]
        PS_ = [pp(f"n_ps{i}", [128, 1024], F32) for i in range(2)]
        PO_t = pp("n_po", [128, 512], F32)
        PO = [PO_t[:, 0:128], PO_t[:, 128:256]]
        PX = [pp(f"n_px{i}", [128, 512], F32) for i in range(2)]

        b_QK = [bufs(NB) for _ in range(4)]
        b_VA = bufs(NT); b_va1 = Buf()
        b_BIAS = Buf(); b_Wt = bufs(2); b_Wv = Buf()
        b_PT = bufs(NR); b_EP = bufs(2); b_YTK = bufs(NT); b_RC = bufs(2)
        b_PS, b_PX = pbufs(2), pbufs(2)
        b_po1 = Buf(True)
        b_PO = [b_po1, b_po1]
        kb.op("pool", lambda e: e.memset(VA[:], 1.0), writes=[b_va1])
        kb.dma("act", Wv[:], s_wnav[l], reads=[wb["nav"][l]], writes=[b_Wv])
        pcnt = 0
        for t4 in range(4):
            wi = t4 % 2
            kb.dma("sp", Wt[wi][:], s_wfm[l, 6 + t4], reads=[wb["fm"][l]], writes=[b_Wt[wi]])
            for blk in range(NB):
                sl = slice(blk * 512, (blk + 1) * 512)
                pi = pcnt % 2; pcnt += 1
                for kc in range(8):
                    kb.op("pe", lambda e, kc=kc: e.matmul(PX[pi][:], lhsT=Wt[wi][:, kc, :], rhs=xT[:, kc, sl], start=(kc == 0), stop=(kc == 7)),
                          reads=[b_Wt[wi], b_xT[blk]], writes=[b_PX[pi]])
                if t4 < 2:
                    kb.op("act", lambda e: e.mul(out=QKt[t4][:, sl], in_=PX[pi][:], mul=0.125), reads=[b_PX[pi]], writes=[b_QK[t4][blk]])
                else:
                    kb.op("dve", lambda e: e.tensor_copy(out=QKt[t4][:, sl], in_=PX[pi][:]), reads=[b_PX[pi]], writes=[b_QK[t4][blk]])
        for tt in range(NT):
            ts_ = slice(tt * 128, (tt + 1) * 128)
            pi = pcnt % 2; pcnt += 1
            for kc in range(8):
                kb.op("pe", lambda e, kc=kc: e.matmul(PX[pi][:, 0:256], lhsT=xT[:, kc, ts_], rhs=Wv[:, kc, :], start=(kc == 0), stop=(kc == 7)),
                      reads=[b_Wv, b_xT[tt // 4]], writes=[b_PX[pi]])
            eng = "act" if tt % 2 == 0 else "dve"
            src = PX[pi][:, 0:256].rearrange("p (h e) -> p h e", h=4)
            if eng == "act":
                kb.op("act", lambda e: e.copy(out=VA[:, tt, :, 0:64], in_=src), reads=[b_PX[pi], b_va1], writes=[b_VA[tt]])
            else:
                kb.op("dve", lambda e: e.tensor_copy(out=VA[:, tt, :, 0:64], in_=src), reads=[b_PX[pi], b_va1], writes=[b_VA[tt]])
        import os
        if os.environ.get("NA_STOP") == "1":
            return
        it = 0
        pending = []
        pending2 = []
        if os.environ.get("NA_STOP") == "3":
            kb.op("pool", lambda e: e.memset(YTK[:], 1.0), writes=b_YTK)
        for h in range(4):
            if os.environ.get("NA_STOP") == "3":
                break
            if os.environ.get("NA_STOP") == "2" and h > 0:
                break
            kb.dma("sp", BIAS[:], na_bias[l, h], writes=[b_BIAS])
            qt = QKt[h // 2]; kt = QKt[2 + h // 2]
            bq = b_QK[h // 2]; bk = b_QK[2 + h // 2]
            hs = slice((h % 2) * 64, (h % 2) * 64 + 64)
            for t in range(NT):
                if os.environ.get("NA_STOP") == "2" and t > 3:
                    break
                s_lo, s_hi = _NA_WIN[t]
                nq = s_hi - s_lo + 1
                N = nq * 128
                pi = it % 2; ri = it % NR; it += 1
                pS = PS_[pi]
                q0 = s_lo * 128
                rdq = [bq[b] for b in range(q0 // 512, (q0 + N - 1) // 512 + 1)]
                for c0 in range(0, N, 512):
                    n = min(512, N - c0)
                    kb.op("pe", lambda e, c0=c0, n=n: e.matmul(pS[:, c0:c0 + n], lhsT=kt[hs, t * 128:(t + 1) * 128], rhs=qt[hs, q0 + c0:q0 + c0 + n], start=True, stop=True),
                          reads=[bk[t // 4]] + rdq, writes=[b_PS[pi]])
                s = s_lo
                while s <= s_hi:
                    b0 = _NA_PAIR[(t, s)]
                    e_ = s
                    while e_ + 1 <= s_hi and _NA_PAIR[(t, e_ + 1)] == b0 + (e_ + 1 - s):
                        e_ += 1
                    n = e_ - s + 1
                    o0 = (s - s_lo) * 128
                    kb.op("dve", lambda e, o0=o0, n=n, b0=b0: e.tensor_tensor(out=EP[pi][:, o0:o0 + n * 128], in0=pS[:, o0:o0 + n * 128],
                                                                            in1=BIAS[:, b0:b0 + n, :].rearrange("p b q -> p (b q)"), op=ALU.add),
                          reads=[b_PS[pi], b_BIAS], writes=[b_EP[pi]])
                    s = e_ + 1
                kb.op("act", lambda e: e.activation(out=PTr[ri][:, 0:N], in_=EP[pi][:, 0:N], func=AF.Exp), reads=[b_EP[pi]], writes=[b_PT[ri]])
                ring_of_t = {tt_: (it - 1 - (t - tt_)) % NR for tt_ in range(max(0, t - NR + 1), t + 1)}
                def pv(t=t, h=h, ring_of_t=ring_of_t):
                    for sq in range(NT):
                        t_lo, t_hi = _NA_KEYT[sq]
                        if t_hi != t or os.environ.get("NA_NOPV") == "1":
                            continue
                        assert t - t_lo < NR - 1
                        po = PO[sq % 2]; bpo = b_PO[sq % 2]
                        for tk in range(t_lo, t_hi + 1):
                            rk = ring_of_t[tk]
                            off = (sq - _NA_WIN[tk][0]) * 128
                            kb.op("pe", lambda e, tk=tk, rk=rk, off=off: e.matmul(po[:, 0:65], lhsT=PTr[rk][:, off:off + 128], rhs=VA[:, tk, h, :], start=(tk == t_lo), stop=(tk == t_hi)),
                                  reads=[b_PT[rk], b_VA[tk]], writes=[bpo])
                        rc = RC[sq % 2]
                        kb.op("dve", lambda e: e.reciprocal(out=rc[:], in_=po[:, 64:65]), reads=[bpo], writes=[b_RC[sq % 2]])
                        kb.op("dve", lambda e: e.tensor_scalar(out=YTK[:, sq, h * 64:(h + 1) * 64], in0=po[:, 0:64], scalar1=rc[:, 0:1], scalar2=None, op0=ALU.mult),
                              reads=[bpo, b_RC[sq % 2]], writes=[b_YTK[sq]])
                for f in pending:
                    f()
                pending = [pv]
            for f in pending:
                f()
            pending = []
        b_ptb1 = Buf(True)
        b_PTb = [b_ptb1, b_ptb1]
        for sq in range(NT):
            pi = sq % 2
            for a in range(2):
                kb.op("pe", lambda e, a=a: e.transpose(PTb[pi][:, a, :], YTK[:, sq, a * 128:(a + 1) * 128], identb[:]), reads=[b_YTK[sq], b_const], writes=[b_PTb[pi]])
            for a in range(2):
                eng = "dve"
                if eng == "act":
                    kb.op("act", lambda e, a=a: e.copy(out=QKt[a][:, sq * 128:(sq + 1) * 128], in_=PTb[pi][:, a, :]), reads=[b_PTb[pi]], writes=[b_QK[a][sq // 4]])
                else:
                    kb.op("dve", lambda e, a=a: e.tensor_copy(out=QKt[a][:, sq * 128:(sq + 1) * 128], in_=PTb[pi][:, a, :]), reads=[b_PTb[pi]], writes=[b_QK[a][sq // 4]])
        for a in range(2):
            kb.dma("sp", yT_d[768 + a * 128:768 + (a + 1) * 128, :], QKt[a][:], reads=b_QK[a], writes=[b_yT_d[6 + a]])


def phase_ffn(nc, kb, l, L, s_wo, s_wg, s_wu, s_wd, wb, ln_gb, xsrc, xdst, b_xres, yT_d, b_yT_d, xT_d, b_xT_d, identf, c_m05, b_const, next_casts=()):
    last = (l == L - 1)
    with contextlib.ExitStack() as ps:
        sb = lambda name, shape, dt: ps.enter_context(nc.sbuf_tensor(f"{name}_L{l}", list(shape), dt))
        pp = lambda name, shape, dt: ps.enter_context(nc.psum_tensor(f"{name}_L{l}", list(shape), dt))
        GB = sb("f_gb", [128, 4, D], F32)
        WO = sb("f_wo", [128, 8, D], BF16)
        WD = sb("f_wd", [128, NF, D], BF16)
        YT = sb("f_yt", [128, 8, 512], BF16)
        XR = [sb(f"f_xr{i}", [128, D], F32) for i in range(2)]
        X1 = sb("f_x1", [128, 4, D], F32)
        X1T = sb("f_x1t", [128, 8, 512], BF16)
        HT = sb("f_ht", [128, NF, 512], BF16)
        NW = 3
        WG = [sb(f"f_wg{i}", [128, 8, 128], BF16) for i in range(NW)]
        WU = [sb(f"f_wu{i}", [128, 8, 128], BF16) for i in range(NW)]
        RB = [sb(f"f_rb{i}", [128, D], F32) for i in range(2)]
        SGT = [sb(f"f_sg{i}", [128, 512], F32) for i in range(2)]
        XO = [sb(f"f_xo{i}", [128, D], F32) for i in range(2)]
        XOT = sb("f_xot", [128, 8, 512], BF16)
        ST6 = [sb(f"f_st6{i}", [128, 2, 6], F32) for i in range(2)]
        MV = [sb(f"f_mv{i}", [128, 2], F32) for i in range(2)]
        RS = [sb(f"f_rs{i}", [128, 2], F32) for i in range(2)]
        PA = [pp(f"f_pa{i}", [128, 2, 512], F32) for i in range(2)]
        PTt = pp("f_pt", [128, 2, 512], F32)
        PG = pp("f_pg", [128, 512], F32)
        PU = pp("f_pu", [128, 512], F32)
        b_GB, b_WO, b_WD, b_YT = Buf(), Buf(), Buf(), Buf()
        b_XR = bufs(2); b_X1 = bufs(4); b_X1T = Buf(); b_HT = bufs(NF)
        b_WG, b_WU = bufs(NW), bufs(NW)
        b_RB, b_XO, b_MV, b_RS = bufs(2), bufs(2), bufs(2), bufs(2)
        b_ST6 = [bufs(2), bufs(2)]
        b_XOT = Buf()
        b_SGT = bufs(2)
        b_PA = pbufs(2)
        b_PTt, b_PG, b_PU = Buf(True), Buf(True), Buf(True)
        kb.dma("sp", GB[:], ln_gb[l], writes=[b_GB])
        kb.dma("act", WO[:], s_wo[l], reads=[wb["wo"][l]], writes=[b_WO])
        kb.dma("act", WD[:], s_wd[l], reads=[wb["wd"][l]], writes=[b_WD])
        yv = yT_d.rearrange("(c p) t -> p c t", p=128)
        xTv = xT_d.rearrange("(kc p) t -> p kc t", p=128)
        fcnt = 0
        lncnt = 0

        def ln_chain(pa_i, resid_ap, resid_bufs, gi, out_ap, out_bufs):
            nonlocal lncnt
            k = lncnt % 2; lncnt += 1
            rb = RB[k]
            kb.op("dve", lambda e: e.scalar_tensor_tensor(out=rb[:], in0=resid_ap, scalar=ALPHA, in1=PA[pa_i][:].rearrange("p a n -> p (a n)"), op0=ALU.mult, op1=ALU.add),
                  reads=resid_bufs + [b_PA[pa_i]], writes=[b_RB[k]])
            for hf in range(2):
                kb.op("dve", lambda e, hf=hf: e.bn_stats(out=ST6[k][:, hf, :], in_=rb[:, hf * 512:(hf + 1) * 512]), reads=[b_RB[k]], writes=[b_ST6[k][hf]])
            kb.op("dve", lambda e: e.bn_aggr(out=MV[k][:], in_=ST6[k][:].rearrange("p a s -> p (a s)")), reads=b_ST6[k], writes=[b_MV[k]])
            kb.op("pool", lambda e: e.tensor_scalar(out=RS[k][:, 0:1], in0=MV[k][:, 1:2], scalar1=LN_EPS, scalar2=None, op0=ALU.add), reads=[b_MV[k]], writes=[b_RS[k]])
            kb.op("pool", lambda e: e.tensor_tensor(out=RS[k][:, 0:1], in0=RS[k][:, 0:1], in1=c_m05[:, 0:1], op=ALU.pow), reads=[b_const], writes=[b_RS[k]])
            kb.op("pool", lambda e: e.tensor_tensor(out=RS[k][:, 1:2], in0=MV[k][:, 0:1], in1=RS[k][:, 0:1], op=ALU.mult), reads=[b_MV[k]], writes=[b_RS[k]])
            kb.op("pool", lambda e: e.tensor_scalar(out=RS[k][:, 1:2], in0=RS[k][:, 1:2], scalar1=-1.0, scalar2=None, op0=ALU.mult), writes=[b_RS[k]])
            kb.op("act", lambda e: e.activation(out=rb[:], in_=rb[:], func=AF.Identity, bias=RS[k][:, 1:2], scale=RS[k][:, 0:1]), reads=[b_RS[k]], writes=[b_RB[k]])
            kb.op("dve", lambda e: e.tensor_tensor(out=rb[:], in0=rb[:], in1=GB[:, gi, :], op=ALU.mult), reads=[b_GB], writes=[b_RB[k]])
            kb.op("pool", lambda e: e.tensor_tensor(out=out_ap, in0=rb[:], in1=GB[:, gi + 1, :], op=ALU.add), reads=[b_GB, b_RB[k]], writes=out_bufs)

        def transposes(src_ap, src_bufs, tdst, tdst_bufs, j):
            for hf in range(2):
                for q in range(4):
                    kc = hf * 4 + q
                    kb.op("pe", lambda e, kc=kc, q=q, hf=hf: e.transpose(PTt[:, hf, q * 128:(q + 1) * 128], src_ap[:, kc * 128:(kc + 1) * 128], identf[:]),
                          reads=src_bufs + [b_const], writes=[b_PTt])
            kb.op("act", lambda e: e.copy(out=tdst[:, :, j * 128:(j + 1) * 128], in_=PTt[:].rearrange("p a (q n) -> p (a q) n", q=4)),
                  reads=[b_PTt], writes=tdst_bufs)

        def load_yt(tb):
            kb.dma("sp", YT[:], yv[:, :, tb * 512:(tb + 1) * 512], reads=b_yT_d, writes=[b_YT])

        def stage_a(tb, j):
            tt = tb * 4 + j
            xi = tt % 2
            kb.dma("act", XR[xi][:], xsrc[tt * 128:(tt + 1) * 128, :], reads=[b_xres[tt]], writes=[b_XR[xi]])
            for hf in range(2):
                for c in range(8):
                    kb.op("pe", lambda e, c=c, hf=hf: e.matmul(PA[1][:, hf, :], lhsT=YT[:, c, j * 128:(j + 1) * 128], rhs=WO[:, c, hf * 512:(hf + 1) * 512], start=(c == 0), stop=(c == 7)),
                          reads=[b_YT, b_WO], writes=[b_PA[1]])
            ln_chain(1, XR[xi][:], [b_XR[xi]], 0, X1[:, j, :], [b_X1[j]])

        def t1(j):
            transposes(X1[:, j, :], [b_X1[j]], X1T, [b_X1T], j)

        def stage_c(tb, j):
            tt = tb * 4 + j
            for hf in range(2):
                for f in range(NF):
                    kb.op("pe", lambda e, f=f, hf=hf: e.matmul(PA[0][:, hf, :], lhsT=HT[:, f, j * 128:(j + 1) * 128], rhs=WD[:, f, hf * 512:(hf + 1) * 512], start=(f == 0), stop=(f == NF - 1)),
                          reads=[b_HT[f], b_WD], writes=[b_PA[0]])
            ln_chain(0, X1[:, j, :], [b_X1[j]], 2, XO[j % 2][:], [b_XO[j % 2]])
            kb.dma("sp", xdst[tt * 128:(tt + 1) * 128, :], XO[j % 2][:], reads=[b_XO[j % 2]], writes=[b_xres[tt]])

        def t2(j):
            if not last:
                transposes(XO[j % 2][:], [b_XO[j % 2]], XOT, [b_XOT], j)

        load_yt(0)
        for j in range(4):
            stage_a(0, j)
            if j >= 1:
                t1(j - 1)
        t1(3)
        for tb in range(NB):
            for jb in list(next_casts)[tb::NB]:
                jb()
            nxt = tb + 1 < NB
            if nxt:
                load_yt(tb + 1)
            for f in range(NF):
                wi = fcnt % NW; pi = fcnt % 2; fcnt += 1
                kb.dma("sp", WG[wi][:], s_wg[l, f], reads=[wb["wg"][l]], writes=[b_WG[wi]])
                kb.dma("act", WU[wi][:], s_wu[l, f], reads=[wb["wu"][l]], writes=[b_WU[wi]])
                for kc in range(8):
                    kb.op("pe", lambda e, kc=kc: e.matmul(PG[:], lhsT=WG[wi][:, kc, :], rhs=X1T[:, kc, :], start=(kc == 0), stop=(kc == 7)),
                          reads=[b_WG[wi], b_X1T], writes=[b_PG])
                for kc in range(8):
                    kb.op("pe", lambda e, kc=kc: e.matmul(PU[:], lhsT=WU[wi][:, kc, :], rhs=X1T[:, kc, :], start=(kc == 0), stop=(kc == 7)),
                          reads=[b_WU[wi], b_X1T], writes=[b_PU])
                kb.op("act", lambda e: e.activation(out=SGT[pi][:], in_=PG[:], func=AF.Silu), reads=[b_PG], writes=[b_SGT[pi]])
                kb.op("dve", lambda e: e.tensor_tensor(out=HT[:, f, :], in0=SGT[pi][:], in1=PU[:], op=ALU.mult), reads=[b_SGT[pi], b_PU], writes=[b_HT[f]])
            for j in range(4):
                stage_c(tb, j)
                if nxt:
                    stage_a(tb + 1, j)
                if j >= 1:
                    t2(j - 1)
                    if nxt:
                        t1(j - 1)
            t2(3)
            if nxt:
                t1(3)
            if not last:
                kb.dma("sp", xTv[:, :, tb * 512:(tb + 1) * 512], XOT[:], reads=[b_XOT], writes=[b_xT_d[tb]])


_NC_CACHE = {}


def _host_inputs(inputs, depth=DEPTH):
    f = lambda a: np.ascontiguousarray(np.asarray(a, dtype=np.float32))
    L = depth
    conv_w = f(inputs["conv_w"])[:L]
    lru_cw = np.ascontiguousarray(conv_w.reshape(L, 4, 3, 128).transpose(0, 2, 3, 1))
    cols = [f(inputs["conv_b"])[:L], f(inputs["lru_b_a"])[:L, 0], f(inputs["lru_b_a"])[:L, 1], f(inputs["lru_b_x"])[:L, 0],
            f(inputs["lru_b_x"])[:L, 1], f(inputs["lru_lam"])[:L, 0], f(inputs["lru_lam"])[:L, 1]]
    lru_vec = np.ascontiguousarray(np.stack([c.reshape(L, 3, 128) for c in cols], axis=-1))
    wa = f(inputs["lru_w_a"])[:L]; wx = f(inputs["lru_w_x"])[:L]
    lru_wg = np.zeros((L, 3, 128, 4, 128), np.float32)
    for ct in range(3):
        for g in range(2):
            sl = slice(g * 64, (g + 1) * 64)
            for d in range(2):
                lru_wg[:, ct, sl, d, sl] = wa[:, d, 2 * ct + g]
                lru_wg[:, ct, sl, 2 + d, sl] = wx[:, d, 2 * ct + g]
    ret_gnw = np.ascontiguousarray(np.broadcast_to(f(inputs["ret_gn_w"])[:L, None, :], (L, 128, 384)))
    na_bias = _na_bias_host(f(inputs["na_rpb"])[:L])
    gb = np.stack([f(inputs["ln1_g"])[:L], f(inputs["ln1_b"])[:L], f(inputs["ln2_g"])[:L], f(inputs["ln2_b"])[:L]], axis=1)
    ln_gb = np.ascontiguousarray(np.broadcast_to(gb[:, None], (L, 128, 4, D)))
    decayT, wtab, gcol, rope = _ret_consts()
    common = {
        "w_in": f(inputs["w_in"])[:L], "w_out": f(inputs["w_out"])[:L], "w_gate": f(inputs["w_gate"])[:L],
        "w_up": f(inputs["w_up"])[:L], "w_down": f(inputs["w_down"])[:L],
        "lru_cw": lru_cw, "lru_vec": lru_vec, "lru_wg": lru_wg, "ret_gnw": ret_gnw, "na_bias": na_bias, "ln_gb": ln_gb,
        "c_ident": np.eye(128, dtype=np.float32), "c_decay": decayT, "c_wtab": wtab, "c_gcol": gcol, "c_rope": rope,
    }
    return common


def kernel(**inputs):
    x = np.asarray(inputs["x"], dtype=np.float32)
    B = x.shape[0]
    common = _host_inputs(inputs)
    if "nc" not in _NC_CACHE:
        _NC_CACHE["nc"] = build(DEPTH)
    nc = _NC_CACHE["nc"]
    in_maps = []
    for b in range(B):
        m = dict(common)
        m["x"] = np.ascontiguousarray(x[b])
        in_maps.append(m)
    res = run_bass_kernel_spmd(nc, in_maps, core_ids=list(range(B)))
    return np.stack([np.asarray(r["out"], dtype=np.float32) for r in res.results], axis=0)
```
